# Optimizing a Trainium2 kernel written in Bass

```python
import jax, jax.numpy as jnp
from jax import lax
import numpy as np

D_MODEL = 1024
BATCH = 4
SEQ = 4096
DEPTH = 1

CONV_DIM = 512
CONV_WIDTH = 3
N_HEADS = 16
N_KV_HEADS = 4
HEAD_DIM = 64
GROUP = N_HEADS // N_KV_HEADS
CMP_LEN = 32
CMP_STRIDE = 16
CMP_HIDDEN = 256
SEL_LEN = 64
SEL_TOP = 16
WINDOW = 512
Q_BLOCK = 64
N_NSA_BRANCHES = 3
D_FF = -(-8 * D_MODEL // (3 * 256)) * 256
EPS = 1e-6

Q_COLS = N_HEADS * HEAD_DIM
KV_COLS = N_KV_HEADS * HEAD_DIM
IN_SIZES = (CONV_DIM, CONV_DIM, CONV_DIM, Q_COLS,
            KV_COLS, KV_COLS, KV_COLS, KV_COLS, KV_COLS, KV_COLS,
            N_NSA_BRANCHES * N_HEADS, D_MODEL, D_MODEL)
IN_COLS = sum(IN_SIZES)

kernel_name = "hybrid_shortconv_nsa_gated_merge"


def _split_points():
    pts, acc = [], 0
    for s in IN_SIZES[:-1]:
        acc += s
        pts.append(acc)
    return pts


def rmsnorm(x, g):
    xf = x.astype(jnp.float32)
    y = xf * lax.rsqrt(jnp.mean(xf * xf, axis=-1, keepdims=True) + EPS)
    return (y * g.astype(jnp.float32)).astype(x.dtype)


def masked_softmax(s, mask):
    s = jnp.where(mask, s.astype(jnp.float32), -jnp.inf)
    m = jnp.max(s, axis=-1, keepdims=True)
    m = jnp.where(jnp.isfinite(m), m, 0.0)
    e = jnp.exp(s - m)
    return e / jnp.maximum(jnp.sum(e, axis=-1, keepdims=True), 1e-30)


def short_conv_mixer(b_gate, c_gate, h_in, conv_w, w_out):
    S = h_in.shape[1]
    u = c_gate * h_in
    up = jnp.pad(u, ((0, 0), (CONV_WIDTH - 1, 0), (0, 0)))
    conv = sum(up[:, k:k + S, :] * conv_w[k] for k in range(CONV_WIDTH))
    return (b_gate * conv) @ w_out


def compress(kv, pos, w1, w2, n_cmp):
    B = kv.shape[0]
    idx = jnp.arange(n_cmp)[:, None] * CMP_STRIDE + jnp.arange(CMP_LEN)[None, :]
    blk = kv[:, idx] + pos[None, None, :, None, :]
    blk = blk.transpose(0, 3, 1, 2, 4).reshape(B, N_KV_HEADS, n_cmp, CMP_LEN * HEAD_DIM)
    return jax.nn.gelu(blk @ w1) @ w2


def overlap_matrix(n_cmp, n_sel):
    cs = np.arange(n_cmp)[:, None] * CMP_STRIDE
    ss = np.arange(n_sel)[None, :] * SEL_LEN
    ov = np.maximum(0, np.minimum(cs + CMP_LEN, ss + SEL_LEN) - np.maximum(cs, ss))
    return jnp.asarray(ov / CMP_LEN, dtype=jnp.float32)


def nsa_attention(q, k_cmp, v_cmp, k_sel, v_sel, k_win, v_win, overlap, n_top):
    B, _, _, S, _ = q.shape
    n_cmp = k_cmp.shape[2]
    n_sel = k_sel.shape[2]
    scale = HEAD_DIM ** -0.5
    cmp_end = jnp.arange(n_cmp) * CMP_STRIDE + CMP_LEN - 1
    blk_ids = jnp.arange(n_sel)
    bi = jnp.arange(B)[:, None, None]
    hi = jnp.arange(N_KV_HEADS)[None, :, None]

    def block(b):
        q0 = b * Q_BLOCK
        qb = lax.dynamic_slice_in_dim(q, q0, Q_BLOCK, axis=3)
        t = q0 + jnp.arange(Q_BLOCK)
        s = jnp.einsum('bhgqd,bhnd->bhgqn', qb, k_cmp) * scale
        p_cmp = masked_softmax(s, cmp_end[None, :] <= t[:, None])
        o_cmp = jnp.einsum('bhgqn,bhnd->bhgqd', p_cmp.astype(v_cmp.dtype), v_cmp)
        imp = jnp.einsum('bhgqn,nm->bhqm', p_cmp, overlap)
        cur = t // SEL_LEN
        forced = (blk_ids[None, :] == 0) | (blk_ids[None, :] == cur[:, None]) | (blk_ids[None, :] == cur[:, None] - 1)
        imp = jnp.where(forced, jnp.inf, imp)
        imp = jnp.where(blk_ids[None, :] * SEL_LEN <= t[:, None], imp, -jnp.inf)
        top_val, top_idx = lax.top_k(imp, n_top)
        blk_ok = top_val > -jnp.inf
        flat = top_idx.reshape(B, N_KV_HEADS, Q_BLOCK * n_top)
        ks = k_sel[bi, hi, flat].reshape(B, N_KV_HEADS, Q_BLOCK, n_top * SEL_LEN, HEAD_DIM)
        vs = v_sel[bi, hi, flat].reshape(B, N_KV_HEADS, Q_BLOCK, n_top * SEL_LEN, HEAD_DIM)
        kpos = top_idx[..., None] * SEL_LEN + jnp.arange(SEL_LEN)
        smask = (kpos <= t[None, None, :, None, None]) & blk_ok[..., None]
        smask = smask.reshape(B, N_KV_HEADS, Q_BLOCK, n_top * SEL_LEN)
        s = jnp.einsum('bhgqd,bhqkd->bhgqk', qb, ks) * scale
        p = masked_softmax(s, smask[:, :, None])
        o_sel = jnp.einsum('bhgqk,bhqkd->bhgqd', p.astype(vs.dtype), vs)
        kw = lax.dynamic_slice_in_dim(k_win, q0, Q_BLOCK + WINDOW, axis=2)
        vw = lax.dynamic_slice_in_dim(v_win, q0, Q_BLOCK + WINDOW, axis=2)
        wpos = q0 - WINDOW + jnp.arange(Q_BLOCK + WINDOW)
        diff = t[:, None] - wpos[None, :]
        wmask = (diff >= 0) & (diff < WINDOW) & (wpos[None, :] >= 0)
        s = jnp.einsum('bhgqd,bhkd->bhgqk', qb, kw) * scale
        p = masked_softmax(s, wmask)
        o_win = jnp.einsum('bhgqk,bhkd->bhgqd', p.astype(vw.dtype), vw)
        return jnp.stack([o_cmp, o_sel, o_win], axis=0)

    out = lax.map(block, jnp.arange(S // Q_BLOCK))
    out = out.transpose(1, 2, 0, 5, 3, 4, 6)
    return out.reshape(N_NSA_BRANCHES, B, S, N_HEADS, HEAD_DIM)


def setup_inputs(seed: int = 0) -> dict:
    key = jax.random.key(seed)
    ks = jax.random.split(key, 20)
    L = DEPTH
    nrm = lambda k, shape, fan_in: jax.random.normal(k, shape, jnp.float32) * (fan_in ** -0.5)
    gain = lambda k, shape: 1.0 + 0.05 * jax.random.normal(k, shape, jnp.float32)
    return {
        "x": jax.random.normal(ks[0], (BATCH, SEQ, D_MODEL), jnp.float32),
        "w_in": nrm(ks[1], (L, D_MODEL, IN_COLS), D_MODEL),
        "conv_w": nrm(ks[2], (L, CONV_WIDTH, CONV_DIM), CONV_WIDTH),
        "w_conv_out": nrm(ks[3], (L, CONV_DIM, D_MODEL), CONV_DIM),
        "cmp_pos_k": 0.1 * jax.random.normal(ks[4], (L, CMP_LEN, HEAD_DIM), jnp.float32),
        "cmp_w1_k": nrm(ks[5], (L, CMP_LEN * HEAD_DIM, CMP_HIDDEN), CMP_LEN * HEAD_DIM),
        "cmp_w2_k": nrm(ks[6], (L, CMP_HIDDEN, HEAD_DIM), CMP_HIDDEN),
        "cmp_pos_v": 0.1 * jax.random.normal(ks[7], (L, CMP_LEN, HEAD_DIM), jnp.float32),
        "cmp_w1_v": nrm(ks[8], (L, CMP_LEN * HEAD_DIM, CMP_HIDDEN), CMP_LEN * HEAD_DIM),
        "cmp_w2_v": nrm(ks[9], (L, CMP_HIDDEN, HEAD_DIM), CMP_HIDDEN),
        "w_attn_out": nrm(ks[10], (L, Q_COLS, D_MODEL), Q_COLS),
        "w_o": nrm(ks[11], (L, D_MODEL, D_MODEL), D_MODEL),
        "g_mix": gain(ks[12], (L, D_MODEL)),
        "g_ffn": gain(ks[13], (L, D_MODEL)),
        "w_gate": nrm(ks[14], (L, D_MODEL, D_FF), D_MODEL),
        "w_up": nrm(ks[15], (L, D_MODEL, D_FF), D_MODEL),
        "w_down": nrm(ks[16], (L, D_FF, D_MODEL), D_FF),
        "g_final": gain(ks[17], (D_MODEL,)),
    }


def reference(x, w_in, conv_w, w_conv_out, cmp_pos_k, cmp_w1_k, cmp_w2_k, cmp_pos_v, cmp_w1_v, cmp_w2_v,
              w_attn_out, w_o, g_mix, g_ffn, w_gate, w_up, w_down, g_final):
    B, S, _ = x.shape
    n_cmp = (S - CMP_LEN) // CMP_STRIDE + 1
    n_sel = S // SEL_LEN
    n_top = min(SEL_TOP, n_sel)
    overlap = overlap_matrix(n_cmp, n_sel)
    split_pts = _split_points()
    h = x
    for l in range(DEPTH):
        n = rmsnorm(h, g_mix[l])
        proj = n @ w_in[l]
        (b_gate, c_gate, h_conv, q, k_c, v_c, k_s, v_s, k_w, v_w,
         g_br, g_conv, g_attn) = jnp.split(proj, split_pts, axis=-1)
        y_conv = short_conv_mixer(b_gate, c_gate, h_conv, conv_w[l], w_conv_out[l])
        q = q.reshape(B, S, N_KV_HEADS, GROUP, HEAD_DIM).transpose(0, 2, 3, 1, 4)
        kvh = lambda t: t.reshape(B, S, N_KV_HEADS, HEAD_DIM)
        k_cmp = compress(kvh(k_c), cmp_pos_k[l], cmp_w1_k[l], cmp_w2_k[l], n_cmp)
        v_cmp = compress(kvh(v_c), cmp_pos_v[l], cmp_w1_v[l], cmp_w2_v[l], n_cmp)
        selb = lambda t: kvh(t).reshape(B, n_sel, SEL_LEN, N_KV_HEADS, HEAD_DIM).transpose(0, 3, 1, 2, 4)
        winp = lambda t: jnp.pad(kvh(t).transpose(0, 2, 1, 3), ((0, 0), (0, 0), (WINDOW, 0), (0, 0)))
        o_all = nsa_attention(q, k_cmp, v_cmp, selb(k_s), selb(v_s), winp(k_w), winp(v_w), overlap, n_top)
        br_gates = jax.nn.sigmoid(g_br).reshape(B, S, N_NSA_BRANCHES, N_HEADS)
        o = jnp.einsum('bsch,cbshd->bshd', br_gates, o_all).reshape(B, S, Q_COLS)
        y_attn = o @ w_attn_out[l]
        mix = jax.nn.sigmoid(g_conv) * y_conv + jax.nn.sigmoid(g_attn) * y_attn
        h = h + mix @ w_o[l]
        n = rmsnorm(h, g_ffn[l])
        h = h + (jax.nn.silu(n @ w_gate[l]) * (n @ w_up[l])) @ w_down[l]
    return rmsnorm(h, g_final)
```

```python
import os
import numpy as np
from contextlib import ExitStack
import concourse.bass as bass
import concourse.mybir as mybir
from concourse.bass_utils import run_bass_kernel_spmd

F32 = mybir.dt.float32
BF16 = mybir.dt.bfloat16
AF = mybir.ActivationFunctionType
ALU = mybir.AluOpType

NCORES = 8
DM = 1024
NCTX = 4096
NOWN = 2048
T = 512
DFF = 2816
NF = DFF // 128
NRING = 12
EPS = 1e-6
KSTOP = int(os.environ.get('KSTOP', '99'))


class Sched:
    ENGS = ("pe", "act", "dve", "pool", "sp")

    def __init__(self):
        self.ops = {e: [] for e in self.ENGS}
        self.count = {e: 0 for e in self.ENGS}
        self.dma_count = {}
        self.last_w = {}
        self.readers = {}
        self.seen = {e: {} for e in self.ENGS}

    def add(self, eng, fn, reads=(), writes=(), dma=None):
        waits = {}
        seen = self.seen[eng]

        def need(tok):
            k, v = tok
            if k == eng and eng == "pe":
                return
            if seen.get(k, 0) >= v:
                return
            if waits.get(k, 0) < v:
                waits[k] = v

        for r in reads:
            t = self.last_w.get(r)
            if t is not None:
                need(t)
        for w in writes:
            t = self.last_w.get(w)
            if t is not None:
                need(t)
            for t in self.readers.get(w, ()):
                need(t)
        for k, v in waits.items():
            seen[k] = v
        if dma is not None:
            n = self.dma_count.get(dma, 0) + 1
            self.dma_count[dma] = n
            tok = (dma, 16 * n)
        else:
            self.count[eng] += 1
            tok = (eng, self.count[eng])
        for r in reads:
            self.readers.setdefault(r, []).append(tok)
        for w in writes:
            self.last_w[w] = tok
            self.readers[w] = []
        self.ops[eng].append((tuple(waits.items()), fn, tok))
        return tok

    def barrier(self, engs):
        cur = {}
        for k in self.ENGS:
            if self.count[k]:
                cur[k] = self.count[k]
        for k, n in self.dma_count.items():
            cur[k] = 16 * n
        for e in engs:
            seen = self.seen[e]
            waits = {}
            for k, v in cur.items():
                if k == e:
                    continue
                if seen.get(k, 0) < v:
                    waits[k] = v
                    seen[k] = v
            if waits:
                self.ops[e].append((tuple(waits.items()), None, None))

    def emit(self, nc, final_eng="sp"):
        keys = list(self.ENGS) + sorted(self.dma_count.keys())
        with ExitStack() as es:
            sems = {}
            for k in keys:
                sems[k] = es.enter_context(nc.semaphore("s_" + str(k)))
            block = es.enter_context(nc.Block())
            final = {}
            for k in keys:
                if k in self.ENGS:
                    if self.count[k]:
                        final[k] = self.count[k]
                else:
                    final[k] = 16 * self.dma_count[k]

            def run(e, name):
                for waits, fn, tok in self.ops[name]:
                    for k, v in waits:
                        e.wait_ge(sems[k], v)
                    if fn is None:
                        continue
                    ins = fn(e)
                    k, v = tok
                    ins.then_inc(sems[k], 1 if k in self.ENGS else 16)
                if name == final_eng:
                    for k, v in final.items():
                        e.wait_ge(sems[k], v)

            @block.sync
            def _(e):
                run(e, "sp")

            @block.tensor
            def _(e):
                run(e, "pe")

            @block.scalar
            def _(e):
                run(e, "act")

            @block.vector
            def _(e):
                run(e, "dve")

            @block.gpsimd
            def _(e):
                run(e, "pool")


def build_program(n_own_mt=4, dbg=None, stage=9, n_kv_mt=8):
    nc = bass.Bass("TRN2", target_bir_lowering=False)
    S = Sched()

    def din(name, shape):
        return nc.dram_tensor(name, list(shape), F32, kind="ExternalInput").ap()

    xc = din("xc", [NCTX, DM])
    wkvF = din("wkvF", [8, 128, 1024])
    wkvT = din("wkvT", [128, 4096])
    wF = din("wF", [88, 128, 1024])
    wbr = din("wbr", [128, 384])
    wco = din("wco", [8, 128, 512])
    w_o = din("w_o", [1024, 1024])
    w_down = din("w_down", [DFF, 1024])
    w1k = din("w1k", [64, 8192])
    w1v = din("w1v", [64, 8192])
    w2k = din("w2k", [128, 256])
    w2v = din("w2v", [128, 128])
    posk = din("posk", [64, 32])
    posv = din("posv", [64, 32])
    convw = din("convw", [128, 12])
    gmixT_d = din("gmixT", [128, 8])
    gffnT_d = din("gffnT", [128, 8])
    gfin_d = din("gfin", [1, 1024])
    ident_d = din("ident", [128, 128])
    eneg_d = din("eneg", [64, 4096])
    ov_d = din("ov", [128, 128])
    tri_d = din("tri", [128, 128])
    win4_d = din("win4", [128, 128])
    cmpm_d = din("cmpm", [128, 4096])
    biasd_d = din("biasd", [128, 1024])
    valid_d = din("valid", [128, 32])
    out = nc.dram_tensor("out", [NOWN, DM], F32, kind="ExternalOutput").ap()
    dbg_out = None
    if dbg is not None:
        dbg_out = nc.dram_tensor("dbg", list(dbg[1]), F32, kind="ExternalOutput").ap()

    with ExitStack() as es:
        def sb(name, shape, dt):
            return es.enter_context(nc.sbuf_tensor(name, list(shape), dt))

        KE = [sb("KE%d" % h, [128, 4096], BF16) for h in range(4)]
        kwZ = [sb("kwZ%d" % h, [128, 2560], BF16) for h in range(4)]
        vsA = sb("vsA", [128, 32 * 260], BF16)
        vwA = sb("vwA", [128, 20 * 260], BF16)
        kcmpZ = sb("kcmpZ", [128, 1024], BF16)
        vcmpA = sb("vcmpA", [128, 2 * 260], BF16)
        ARENA = sb("ARENA", [128, 16384], BF16)
        WKV = sb("WKV", [128, 12288], BF16)
        xt = sb("xt", [128, 4 * 1024], F32)
        xnT = sb("xnT", [128, 8 * 512], BF16)
        xn = sb("xn", [128, 1024], BF16)
        AR2 = sb("AR2", [128, 8192], BF16)
        ss = sb("ss", [128, 4], F32)
        rstd = sb("rstd", [128, 4], F32)
        gates = sb("gates", [128, 4 * 48], F32)
        den = sb("den", [128, 4], F32)
        rden = sb("rden", [128, 4], F32)
        coef = sb("coef", [128, 4], F32)
        impv = sb("impv", [128, 64], F32)
        v2 = sb("v2", [128, 64], F32)
        m1 = sb("m1", [128, 8], F32)
        m2 = sb("m2", [128, 8], F32)
        thr = sb("thr", [128, 1], F32)
        selp = sb("selp", [128, 128], BF16)
        uprev = sb("uprev", [128, 8], F32)
        xnTh = sb("xnTh", [128, 16], BF16)
        cSh = sb("cSh", [128, 2], F32)
        biasT = sb("biasT", [128, 2], F32)
        geT2 = [sb("geT%d" % i, [128, 512], BF16) for i in range(2)]
        FS = sb("FS", [128, 2562], F32)
        cS = FS[:, 0:512]
        cv = FS[:, 512:1024]
        ub = FS[:, 1024:1538]
        mtmp = FS[:, 1538:2050]
        outb = FS[:, 0:1024]
        gx = FS[:, 0:256]
        gy = FS[:, 512:768]
        gs = FS[:, 1024:1280]
        oaccs = [sb("oacc%d" % i, [128, 1024], BF16) for i in range(2)]
        tmpO = sb("tmpO", [128, 256], F32)
        identb = sb("identb", [128, 128], BF16)
        ovb = sb("ovb", [128, 128], BF16)
        trib = sb("trib", [128, 128], BF16)
        win4b = sb("win4b", [128, 128], BF16)
        biasd = sb("biasd_s", [128, 256], F32)
        validb = sb("validb", [128, 32], F32)
        convwb = sb("convwb", [128, 12], F32)
        gmixT = sb("gmixT_s", [128, 8], F32)
        gffnT = sb("gffnT_s", [128, 8], F32)
        gfin = sb("gfin_s", [128, 1024], F32)
        wbrb = sb("wbrb", [128, 384], BF16)
        w2kb = sb("w2kb", [128, 256], BF16)
        w2vb = sb("w2vb", [128, 128], BF16)
        poskb = sb("poskb", [64, 32], BF16)
        posvb = sb("posvb", [64, 32], BF16)

        pb = [es.enter_context(nc.psum_tensor("pb%d" % i, [128, 512], F32)) for i in range(8)]
        pT = pb[3][:, :].bitcast(BF16)

        def dma_cast(dst, src, key, n, extra_w=()):
            if n <= 2048:
                S.add("pool", lambda e: e.dma_start(out=dst, in_=src), writes=[key] + list(extra_w), dma=key)
            else:
                a = n // 2048
                d3 = dst.rearrange("p (a b) -> p a b", a=a)
                s3 = src.rearrange("p (a b) -> p a b", a=a)
                S.add("pool", lambda e: e.dma_start(out=d3, in_=s3), writes=[key] + list(extra_w), dma=key)

        def dma_sp(dst, src, key, reads=(), writes=()):
            S.add("sp", lambda e: e.dma_start(out=dst, in_=src), reads=list(reads), writes=list(writes), dma=key)

        def mm(o, lhsT, rhs, start, stop, reads, writes, sgc=False):
            S.add("pe", lambda e: e.matmul(o, lhsT=lhsT, rhs=rhs, start=start, stop=stop, skip_group_check=sgc),
                  reads=reads, writes=writes)

        def tr(o, i, reads, writes):
            S.add("pe", lambda e: e.transpose(out=o, in_=i, identity=identb[0:i.shape[0], 0:i.shape[0]]),
                  reads=list(reads) + ["identb"], writes=writes)

        def act(o, i, func, reads, writes, **kw):
            S.add("act", lambda e: e.activation(out=o, in_=i, func=func, **kw), reads=reads, writes=writes)

        def tt(o, a, b, op, reads, writes, eng="dve"):
            S.add(eng, lambda e: e.tensor_tensor(out=o, in0=a, in1=b, op=op), reads=reads, writes=writes)

        def ts(o, a, s1, s2, op0, op1, reads, writes, eng="dve"):
            if s2 is None:
                S.add(eng, lambda e: e.tensor_scalar(out=o, in0=a, scalar1=s1, scalar2=None, op0=op0),
                      reads=reads, writes=writes)
            else:
                S.add(eng, lambda e: e.tensor_scalar(out=o, in0=a, scalar1=s1, scalar2=s2, op0=op0, op1=op1),
                      reads=reads, writes=writes)

        def stt(o, a, sc, b, op0, op1, reads, writes):
            S.add("dve", lambda e: e.scalar_tensor_tensor(out=o, in0=a, scalar=sc, in1=b, op0=op0, op1=op1),
                  reads=reads, writes=writes)

        def cp(o, i, reads, writes, eng="dve"):
            if eng == "act":
                S.add("act", lambda e: e.copy(out=o, in_=i), reads=reads, writes=writes)
            else:
                S.add(eng, lambda e: e.tensor_copy(out=o, in_=i), reads=reads, writes=writes)

        def memset(ap, val, key, eng="pool"):
            S.add(eng, lambda e: e.memset(ap, val), writes=[key])

        bank_ctr = [0]

        def next_bank():
            b = (0, 1, 2, 6, 7)[bank_ctr[0] % 5]
            bank_ctr[0] += 1
            return b

        evac_ctr = [0]

        def evac_eng():
            evac_ctr[0] += 1
            return "act" if evac_ctr[0] % 2 else "dve"

        def v3(ap, a):
            return ap.rearrange("p (a b) -> p a b", a=a)

        dma_cast(identb[:, :], ident_d, "identb", 128)
        dma_sp(gmixT[:, :], gmixT_d, "gmixT", writes=["gmixT"])
        for j in range(8):
            dma_cast(WKV[:, j * 1024:(j + 1) * 1024], wkvF[j], "wkvF%d" % j, 1024)
        dma_cast(WKV[:, 8192:12288], wkvT, "wkvT", 4096)
        dma_sp(validb[:, :], valid_d, "validb", writes=["validb"])
        for h in range(4):
            memset(kwZ[h][:, :], 0.0, "kwT")
        memset(kcmpZ[:, :], 0.0, "kcmpT")
        memset(vcmpA[:, :], 1.0, "vcmpA")
        memset(geT2[0][:, :], 0.0, "geT0")
        memset(geT2[1][:, :], 0.0, "geT1")
        memset(uprev[:, :], 0.0, "uprev")
        memset(selp[:, :], 0.0, "selp")
        dma_cast(w2kb[:, :], w2k, "w2kb", 256)
        dma_cast(w2vb[:, :], w2v, "w2vb", 128)
        dma_cast(poskb[:, :], posk, "poskb", 32)
        dma_cast(posvb[:, :], posv, "posvb", 32)
        for h in range(4):
            oh_ = 1 - (h % 2)
            dma_cast(KE[h][64 * oh_:64 * oh_ + 64, :], eneg_d, "KEe%d" % h, 4096)
        dma_cast(ovb[:, :], ov_d, "ovb", 128)
        dma_cast(trib[:, :], tri_d, "trib", 128)
        dma_cast(win4b[:, :], win4_d, "win4b", 128)
        dma_cast(wbrb[:, :], wbr, "wbrb", 384)
        dma_sp(convwb[:, :], convw, "convwb", writes=["convwb"])
        dma_sp(gffnT[:, :], gffnT_d, "gffnT", writes=["gffnT"])
        dma_sp(gfin[:, :], gfin_d.partition_broadcast(128), "gfin", writes=["gfin"])
        vs4 = vsA[:, :].rearrange("p (c h d) -> p c h d", c=32, h=4)
        vw4 = vwA[:, :].rearrange("p (c h d) -> p c h d", c=20, h=4)
        for h in range(4):
            cp(vs4[:, :, h, 64], validb[:, :], ["validb"], ["vsA"])
            cp(vw4[:, :, h, 64], validb[:, 12:32], ["validb"], ["vwA"])

        def xn_p1(s):
            xs = xt[:, s * 1024:(s + 1) * 1024]
            memset(ss[:, s:s + 1], 0.0, "ss", eng="dve")
            act(FS[:, 0:1024], xs, AF.Square, ["xt%d" % s], ["sqjunk", "ss"], accum_out=ss[:, s:s + 1])
            act(rstd[:, s:s + 1], ss[:, s:s + 1], AF.Sqrt, ["ss"], ["rstd"], bias=EPS, scale=1.0 / DM)
            S.add("dve", lambda e, s=s: e.reciprocal(out=rstd[:, s:s + 1], in_=rstd[:, s:s + 1]),
                  reads=["rstd"], writes=["rstd"])
            ts(xn[:, :], xs, rstd[:, s:s + 1], None, ALU.mult, None, ["xt%d" % s, "rstd"], ["xn"])

        def xn_p2(s, gT, gkey, dstT=None, dkey="xnT"):
            if dstT is None:
                dstT = xnT[:, :]
            for kc in range(8):
                tr(pT[:, kc * 128:(kc + 1) * 128], xn[:, kc * 128:(kc + 1) * 128], ["xn"], ["pb3"])
            tt(v3(dstT, 8)[:, :, s * 128:(s + 1) * 128], v3(pT, 8),
               gT[:, :].unsqueeze(2).to_broadcast([128, 8, 128]), ALU.mult, ["pb3", gkey], [dkey])

        def xn_sub(s, gT, gkey, dstT=None, dkey="xnT"):
            xn_p1(s)
            xn_p2(s, gT, gkey, dstT, dkey)

        def make_xnT(gT, gkey):
            for s in range(4):
                xn_sub(s, gT, gkey)

        def proj_F(wtile, wkey, rhs_of, rkeys, K=8):
            b = next_bank()
            for kc in range(K):
                mm(pb[b][:, :], wtile[:, kc * 128:(kc + 1) * 128], rhs_of(kc), kc == 0, kc == K - 1,
                   [wkey] + rkeys, ["pb%d" % b])
            return b

        xnT_kc = lambda kc: xnT[:, kc * 512:(kc + 1) * 512]

        kvdst = [(None, 4096, 0, 0), (None, 4096, 1, 0), ("kw", 2560, 0, 1536), ("kw", 2560, 1, 1536),
                 (ARENA, 4096, 0, 0), (ARENA, 4096, 1, 0), (ARENA, 4096, 2, 0), (ARENA, 4096, 3, 0)]
        kvkeys = ["ksT", "ksT", "kwT", "kwT", "kcT", "kcT", "vcT", "vcT"]
        xnT_bufs = [(AR2[:, 0:4096], "xnTB"), (xnT[:, :], "xnT")]

        def kv_chain_p1(m, s_):
            dma_sp(xt[:, s_ * 1024:(s_ + 1) * 1024], xc[m * 512 + s_ * 128: m * 512 + (s_ + 1) * 128, :],
                   "xt%d" % s_, writes=["xt%d" % s_])
            xn_p1(s_)

        def kv_chain_p2(m, s_):
            dstT, dkey = xnT_bufs[m % 2]
            xn_p2(s_, gmixT, "gmixT", dstT, dkey)

        def kv_chain_sub(m, s_):
            kv_chain_p1(m, s_)
            kv_chain_p2(m, s_)

        def kv_F(m, j):
            xb, xk = xnT_bufs[m % 2]
            buf, width, ci, t0 = kvdst[j]
            if m * 512 < t0:
                return
            b = proj_F(WKV[:, j * 1024:(j + 1) * 1024], "wkvF%d" % j, lambda kc: xb[:, kc * 512:(kc + 1) * 512], [xk])
            col = ci * width + m * 512 - t0
            if buf is None:
                ee = evac_eng()
                cp(KE[2 * ci][0:64, m * 512:(m + 1) * 512], pb[b][0:64, :], ["pb%d" % b], ["ksT"], eng=ee)
                cp(KE[2 * ci + 1][64:128, m * 512:(m + 1) * 512], pb[b][64:128, :], ["pb%d" % b], ["ksT"], eng=ee)
            elif buf == "kw":
                ee = evac_eng()
                c0 = m * 512 - t0
                cp(kwZ[2 * ci][0:64, c0:c0 + 512], pb[b][0:64, :], ["pb%d" % b], ["kwT"], eng=ee)
                cp(kwZ[2 * ci + 1][64:128, c0:c0 + 512], pb[b][64:128, :], ["pb%d" % b], ["kwT"], eng=ee)
            else:
                dst = ARENA[:, ci * 4096:(ci + 1) * 4096].rearrange("p (r i) -> p r i", r=16)[:, :, m * 32:(m + 1) * 32]
                src = pb[b][:, :].rearrange("p (i r) -> p r i", r=16)
                cp(dst, src, ["pb%d" % b], [kvkeys[j]], eng=evac_eng())

        def kv_T(m, s):
            xb, xk = xnT_bufs[m % 2]
            b = next_bank()
            for kc in range(8):
                mm(pb[b][:, :], xb[:, kc * 512 + s * 128: kc * 512 + (s + 1) * 128],
                   WKV[:, 8192 + kc * 512: 8192 + (kc + 1) * 512], kc == 0, kc == 7,
                   [xk, "wkvT"], ["pb%d" % b])
            c = 4 * m + s
            ee = evac_eng()
            cp(vs4[:, c, :, 0:64], v3(pb[b][:, 0:256], 4), ["pb%d" % b], ["vsA"], eng=ee)
            if c >= 12:
                cp(vw4[:, c - 12, :, 0:64], v3(pb[b][:, 256:512], 4), ["pb%d" % b], ["vwA"], eng=ee)

        for s_ in range(4 if n_kv_mt else 0):
            kv_chain_sub(0, s_)
        for m in range(n_kv_mt):
            if m == 3:
                xb, xk = xnT_bufs[m % 2]
                cp(v3(xnTh[:, :], 8), v3(xb, 8)[:, :, 510:512], [xk], ["xnTh"])
            work = [("F", j) for j in range(8)] + [("T", s_) for s_ in range(4)]
            for qtr in range(4):
                if m + 1 < n_kv_mt:
                    kv_chain_p1(m + 1, qtr)
                for kind_, idx_ in work[qtr * 3:(qtr + 1) * 3]:
                    if kind_ == "F":
                        kv_F(m, idx_)
                    else:
                        kv_T(m, idx_)
                if m + 1 < n_kv_mt:
                    kv_chain_p2(m + 1, qtr)

        vc4 = vcmpA[:, :].rearrange("p (c h d) -> p c h d", c=2, h=4)
        for kv in range(2 if stage >= 2 else 0):
            w1src = w1k if kv == 0 else w1v
            posb = poskb if kv == 0 else posvb
            poskey = "poskb" if kv == 0 else "posvb"
            srcoff = 0 if kv == 0 else 8192
            srckey = "kcT" if kv == 0 else "vcT"
            W1 = AR2 if kv == 0 else WKV
            w1ka, w1kb = ("w1a", "w1b") if kv == 0 else ("w1c", "w1d")
            xw = ["xnTB"] if kv == 0 else ["wkvF%d" % j_ for j_ in range(8)]
            dma_cast(W1[0:64, 0:8192], w1src, w1ka, 8192, extra_w=xw)
            dma_cast(W1[64:128, 0:8192], w1src, w1kb, 8192, extra_w=xw)
            for hc in range(2):
                b = next_bank()
                for l in range(32):
                    mm(pb[b][:, 0:1], W1[0:64, l * 256 + hc * 128: l * 256 + (hc + 1) * 128], posb[:, l:l + 1],
                       l == 0, l == 31, [w1ka, poskey], ["pb%d" % b])
                cp(biasT[:, hc:hc + 1], pb[b][:, 0:1], ["pb%d" % b], ["biasT"])
            for pair in range(2):
                hbanks = {}
                for hc in range(2):
                    bs = [next_bank(), next_bank()]
                    for l in range(32):
                        for half in range(2):
                            prt = slice(64 * half, 64 * half + 64)
                            w1key = w1ka if half == 0 else w1kb
                            base = srcoff + pair * 4096 + (l % 16) * 256 + (l // 16)
                            rhs = ARENA[prt, base: base + 255]
                            mm(pb[bs[half]][:, 0:255], W1[prt, l * 256 + hc * 128: l * 256 + (hc + 1) * 128], rhs,
                               l == 0, l == 31, [w1key, srckey], ["pb%d" % bs[half]])
                    hbanks[hc] = bs
                    G = [(FS[:, 0:256], FS[:, 512:768], FS[:, 1024:1280], "gx0", "gy0", "gs0"),
                         (FS[:, 256:512], FS[:, 768:1024], FS[:, 1280:1536], "gx1", "gy1", "gs1")]
                    for half in range(2):
                        gx_, gy_, gs_, kx, ky, ks_ = G[half]
                        b = bs[half]
                        ts(gx_[:, 0:255], pb[b][:, 0:255], biasT[:, hc:hc + 1], None, ALU.add, None,
                           ["pb%d" % b, "biasT"], [kx])
                    for half in range(2):
                        gx_, gy_, gs_, kx, ky, ks_ = G[half]
                        tt(gy_[:, 0:255], gx_[:, 0:255], gx_[:, 0:255], ALU.mult, [kx], [ky])
                    for half in range(2):
                        gx_, gy_, gs_, kx, ky, ks_ = G[half]
                        ts(gy_[:, 0:255], gy_[:, 0:255], 0.044715, 1.0, ALU.mult, ALU.add, [ky], [ky])
                    for half in range(2):
                        gx_, gy_, gs_, kx, ky, ks_ = G[half]
                        tt(gy_[:, 0:255], gy_[:, 0:255], gx_[:, 0:255], ALU.mult, [ky, kx], [ky])
                    for half in range(2):
                        gx_, gy_, gs_, kx, ky, ks_ = G[half]
                        act(gs_[:, 0:255], gy_[:, 0:255], AF.Sigmoid, [ky], [ks_], scale=1.5957691216057308)
                    for half in range(2):
                        gx_, gy_, gs_, kx, ky, ks_ = G[half]
                        tt(geT2[half][:, hc * 256: hc * 256 + 255], gx_[:, 0:255], gs_[:, 0:255], ALU.mult,
                           [kx, ks_], ["geT%d" % half])
                for half in range(2):
                    h = 2 * pair + half
                    prt = slice(64 * half, 64 * half + 64)
                    geTh = geT2[half]
                    gkey = "geT%d" % half
                    if kv == 0:
                        b = next_bank()
                        for hc in range(2):
                            mm(pb[b][:, 0:256], w2kb[:, hc * 128:(hc + 1) * 128], geTh[:, hc * 256:(hc + 1) * 256],
                               hc == 0, hc == 1, ["w2kb", gkey], ["pb%d" % b])
                        cp(kcmpZ[prt, h * 256:(h + 1) * 256], pb[b][prt, 0:256], ["pb%d" % b], ["kcmpT"])
                    else:
                        for bc in range(2):
                            b = next_bank()
                            for hc in range(2):
                                mm(pb[b][:, 0:64], geTh[:, hc * 256 + bc * 128: hc * 256 + (bc + 1) * 128],
                                   w2vb[:, hc * 64:(hc + 1) * 64], hc == 0, hc == 1, ["w2vb", gkey], ["pb%d" % b])
                            cp(vc4[:, bc, h, 0:64], pb[b][:, 0:64], ["pb%d" % b], ["vcmpA"])

        S.barrier(["pe", "act", "dve", "pool"])

        QT_OFF, OT_OFF, MIX_OFF, ACT_OFF = 0, 8192, 12288, 0
        PT = [AR2[:, i * 512:(i + 1) * 512] for i in range(3)] + [AR2[:, 4096:4608], AR2[:, 4608:5120]]
        sg1 = AR2[:, 1536:2048]
        sg2 = AR2[:, 2048:2560]
        sil = AR2[:, 2560:3072]
        cmpm_t = AR2[:, 3072:4096]
        obf = AR2[:, 4096:5120]
        bcTb = AR2[:, 5120:7168]
        ring = [WKV[:, i * 1024:(i + 1) * 1024] for i in range(NRING)]
        ring_ctr = [0]

        def ring_load(src, n=1024):
            i = ring_ctr[0] % NRING
            ring_ctr[0] += 1
            dma_cast(ring[i][:, 0:n], src, "ring%d" % i, n)
            return ring[i], "ring%d" % i

        pt_ctr = [0]
        ob_ctr = [0]
        pm_ctr = [0]

        for mt in range(n_own_mt if stage >= 3 else 0):
            if mt > 0:
                S.barrier(["pe", "act", "dve"])
            m = 4 + mt
            for s_ in range(4):
                dma_sp(xt[:, s_ * 1024:(s_ + 1) * 1024], xc[m * 512 + s_ * 128: m * 512 + (s_ + 1) * 128, :],
                       "xt%d" % s_, writes=["xt%d" % s_])
            dma_cast(v3(cmpm_t, 2), v3(cmpm_d, 2)[:, :, mt * 512:(mt + 1) * 512], "cmpm_t", 512)
            dma_sp(biasd[:, :], biasd_d[:, mt * 256:(mt + 1) * 256], "biasd", writes=["biasd"])
            make_xnT(gmixT, "gmixT")
            for ci in range(8):
                wt, wk = ring_load(wF[ci])
                b = proj_F(wt, wk, xnT_kc, ["xnT"])
                pair, g = ci // 4, ci % 4
                ee = evac_eng()
                for hh in range(2):
                    h_ = 2 * pair + hh
                    pr_ = slice(64 * hh, 64 * hh + 64)
                    dst = ARENA[pr_, QT_OFF + h_ * 2048: QT_OFF + (h_ + 1) * 2048].rearrange(
                        "p (q g t) -> p q g t", q=4, g=4)[:, :, g, :]
                    cp(dst, v3(pb[b][pr_, :], 4), ["pb%d" % b], ["QT"], eng=ee)
            for cc in range(4):
                wc, wck = ring_load(wF[8 + cc * 3 + 0])
                wh, whk = ring_load(wF[8 + cc * 3 + 1])
                wb_, wbk = ring_load(wF[8 + cc * 3 + 2])
                bc_ = proj_F(wc, wck, xnT_kc, ["xnT"])
                if mt == 0:
                    for kc in range(8):
                        mm(pb[3][:, 256:258], wc[:, kc * 128:(kc + 1) * 128], xnTh[:, kc * 2:(kc + 1) * 2],
                           kc == 0, kc == 7, [wck, "xnTh"], ["pb3"])
                    cp(cSh[:, :], pb[3][:, 256:258], ["pb3"], ["cSh"], eng="act")
                cp(cS[:, :], pb[bc_][:, :], ["pb%d" % bc_], ["cS"], eng="act")
                bh_ = proj_F(wh, whk, xnT_kc, ["xnT"])
                if mt == 0:
                    for kc in range(8):
                        mm(pb[3][:, 264:266], wh[:, kc * 128:(kc + 1) * 128], xnTh[:, kc * 2:(kc + 1) * 2],
                           kc == 0, kc == 7, [whk, "xnTh"], ["pb3"])
                    tt(ub[:, 0:2], cSh[:, :], pb[3][:, 264:266], ALU.mult, ["cSh", "pb3"], ["ub"])
                else:
                    cp(ub[:, 0:2], uprev[:, cc * 2:(cc + 1) * 2], ["uprev"], ["ub"])
                tt(ub[:, 2:514], cS[:, :], pb[bh_][:, :], ALU.mult, ["cS", "pb%d" % bh_], ["ub"])
                cp(uprev[:, cc * 2:(cc + 1) * 2], ub[:, 512:514], ["ub"], ["uprev"])
                ts(cv[:, :], ub[:, 0:512], convwb[:, cc * 3 + 0: cc * 3 + 1], None, ALU.mult, None,
                   ["ub", "convwb"], ["cv"])
                stt(cv[:, :], ub[:, 1:513], convwb[:, cc * 3 + 1: cc * 3 + 2], cv[:, :], ALU.mult, ALU.add,
                    ["ub", "cv", "convwb"], ["cv"])
                stt(cv[:, :], ub[:, 2:514], convwb[:, cc * 3 + 2: cc * 3 + 3], cv[:, :], ALU.mult, ALU.add,
                    ["ub", "cv", "convwb"], ["cv"])
                bb_ = proj_F(wb_, wbk, xnT_kc, ["xnT"])
                tt(bcTb[:, cc * 512:(cc + 1) * 512], cv[:, :], pb[bb_][:, :], ALU.mult,
                   ["cv", "pb%d" % bb_], ["bcT"])
            for s in range(4):
                b = next_bank()
                for kc in range(8):
                    mm(pb[b][:, 0:48], xnT[:, kc * 512 + s * 128: kc * 512 + (s + 1) * 128],
                       wbrb[:, kc * 48:(kc + 1) * 48], kc == 0, kc == 7, ["xnT", "wbrb"], ["pb%d" % b])
                act(gates[:, s * 48:(s + 1) * 48], pb[b][:, 0:48], AF.Sigmoid, ["pb%d" % b], ["gates"])

            items = []
            groups = []
            for qi in range(4 if stage >= 4 else 0):
                j = 4 * mt + qi
                Dg = 16 + j
                for h in range(4):
                    groups.append((qi, j, Dg, h))

            def add_branch(gr, br, kind):
                qi, j, Dg, h = gr
                if kind == "cmp":
                    chunks = [0, 1]
                elif kind == "sel":
                    chunks = list(range(0, Dg + 1))
                else:
                    chunks = list(range(Dg - 4, Dg + 1))
                ob = 4 + (ob_ctr[0] % 2)
                ob_ctr[0] += 1
                for c in chunks:
                    items.append(dict(qi=qi, j=j, Dg=Dg, h=h, br=br, kind=kind, c=c, ob=ob,
                                      first=(c == chunks[0]), last=(c == chunks[-1]),
                                      qlast=(h == 3 and kind == "sel" and c == chunks[-1])))

            for gi, gr in enumerate(groups):
                if gi == 0:
                    add_branch(gr, 2, "win")
                    add_branch(gr, 0, "cmp")
                    items[-1]["fireB"] = (gr[0], gr[3])
                if gi + 1 < len(groups):
                    add_branch(groups[gi + 1], 2, "win")
                    add_branch(groups[gi + 1], 0, "cmp")
                    nxt = groups[gi + 1]
                else:
                    nxt = None
                n0 = len(items)
                add_branch(gr, 1, "sel")
                if nxt is not None:
                    items[min(n0 + 10, len(items) - 1)]["fireB"] = (nxt[0], nxt[3])

            def it_setup(it):
                h, c, kind, qi = it["h"], it["c"], it["kind"], it["qi"]
                half, pair = h % 2, h // 2
                prt = slice(64 * half, 64 * half + 64)
                it["prt"] = prt
                qoff = QT_OFF + h * 2048 + qi * 512
                it["rhsQ"] = ARENA[:, qoff: qoff + 512]
                it["qkeys"] = ["QT", "NS%d_%d" % (h, qi)]
                if kind == "cmp":
                    it["lhsT"] = kcmpZ[:, h * 256 + c * 128: h * 256 + (c + 1) * 128]
                    it["kkey"] = "kcmpT"
                    it["vaug"] = vc4[:, c, h, :]
                    it["vkey"] = "vcmpA"
                elif kind == "sel":
                    it["lhsT"] = KE[h][:, c * 128:(c + 1) * 128]
                    it["qkeys"] = ["QT", "NS%d_%d" % (h, qi), "KEe%d" % h]
                    it["kkey"] = "ksT"
                    it["vaug"] = vs4[:, c, h, :]
                    it["vkey"] = "vsA"
                else:
                    it["lhsT"] = kwZ[h][:, (c - 12) * 128: (c - 11) * 128]
                    it["kkey"] = "kwT"
                    it["vaug"] = vw4[:, c - 12, h, :]
                    it["vkey"] = "vwA"

            def emit_S(it):
                it_setup(it)
                b = next_bank()
                it["b"] = b
                mm(pb[b][:, :], it["lhsT"], it["rhsQ"], True, True, [it["kkey"]] + it["qkeys"], ["pb%d" % b])

            def emit_M(it):
                it["mask"] = None
                kind, c, Dg, qi = it["kind"], it["c"], it["Dg"], it["qi"]
                if kind == "cmp":
                    it["mask"] = cmpm_t[:, c * 512 + qi * 128: c * 512 + (qi + 1) * 128]
                    it["mkey"] = "cmpm_t"
                elif kind == "sel":
                    if c == Dg:
                        it["mask"], it["mkey"] = trib[:, :], "trib"
                else:
                    o = Dg - c
                    if o == 0:
                        it["mask"], it["mkey"] = trib[:, :], "trib"
                    elif o == 4:
                        it["mask"], it["mkey"] = win4b[:, :], "win4b"

            def emit_E(it):
                pi = pt_ctr[0] % 5
                pt_ctr[0] += 1
                P = PT[pi]
                pkey = "PT%d" % pi
                it["P"], it["pkey"] = P, pkey
                act(P, pb[it["b"]][:, :], AF.Exp, ["pb%d" % it["b"]], [pkey], scale=0.125)
                if it["mask"] is not None:
                    tt(v3(P, 4), v3(P, 4), it["mask"].unsqueeze(1).to_broadcast([128, 4, 128]), ALU.mult,
                       [pkey, it["mkey"]], [pkey])

            def emit_PV(it):
                kind, c, qi, h, br, j = it["kind"], it["c"], it["qi"], it["h"], it["br"], it["j"]
                ob = it["ob"]
                okey = "pb%d" % ob
                P, pkey = it["P"], it["pkey"]
                first, last = it["first"], it["last"]
                O3 = v3(pb[ob][:, 0:260], 4)
                for g in range(4):
                    mm(pb[ob][:, g * 65:(g + 1) * 65], P[:, g * 128:(g + 1) * 128], it["vaug"],
                       first and g == 0, last and g == 3, [pkey, it["vkey"]], [okey], sgc=True)
                if kind == "cmp":
                    for g in range(4):
                        mm(pb[3][:, 256 + g * 64: 256 + (g + 1) * 64], P[:, g * 128:(g + 1) * 128],
                           ovb[:, c * 64:(c + 1) * 64], first and g == 0, last and g == 3,
                           [pkey, "ovb"], ["pb3"], sgc=True)
                if not last:
                    return

                def post():
                    ts(den[:, :], O3[:, :, 64], 1e-30, None, ALU.max, None, [okey], ["den"])
                    S.add("dve", lambda e: e.reciprocal(out=rden[:, :], in_=den[:, :]), reads=["den"], writes=["rden"])
                    if kind == "cmp":
                        ts(impv[:, :], pb[3][:, 256:320], rden[:, 0:1], None, ALU.mult, None, ["pb3", "rden"], ["impv"])
                        for g in range(1, 4):
                            stt(impv[:, :], pb[3][:, 256 + g * 64: 256 + (g + 1) * 64], rden[:, g:g + 1], impv[:, :],
                                ALU.mult, ALU.add, ["pb3", "rden", "impv"], ["impv"])
                        tt(impv[:, :], impv[:, :], biasd[:, qi * 64:(qi + 1) * 64], ALU.add, ["impv", "biasd"], ["impv"])
                        S.add("dve", lambda e: e.max(out=m1[:, :], in_=impv[:, :]), reads=["impv"], writes=["m1"])
                        S.add("dve", lambda e: e.match_replace(out=v2[:, :], in_to_replace=m1[:, :],
                                                               in_values=impv[:, :], imm_value=-3.0e38),
                              reads=["impv", "m1"], writes=["v2"])
                        S.add("dve", lambda e: e.max(out=m2[:, :], in_=v2[:, :]), reads=["v2"], writes=["m2"])
                        ts(thr[:, :], m2[:, 7:8], -1.0e29, None, ALU.max, None, ["m2"], ["thr"])
                        oh_ = 1 - (h % 2)
                        opr = slice(64 * oh_, 64 * oh_ + 64)
                        ts(selp[:, 64 * oh_:64 * oh_ + 64], impv[:, :], thr[:, 0:1], None, ALU.is_lt, None,
                           ["impv", "thr"], ["selp"])
                        qoff = QT_OFF + h * 2048 + qi * 512

                        def partB(opr=opr, qoff=qoff, h=h, qi=qi):
                            tr(pT[:, 0:128], selp[:, :], ["selp"], ["pb3"])
                            cp(v3(ARENA[opr, qoff: qoff + 512], 4),
                               pT[opr, 0:128].unsqueeze(1).to_broadcast([64, 4, 128]),
                               ["pb3"], ["NS%d_%d" % (h, qi)])
                        pendingB[(qi, h)] = partB
                    gsl = gates[:, qi * 48 + br * 16 + h * 4: qi * 48 + br * 16 + h * 4 + 4]
                    tt(coef[:, :], gsl, rden[:, :], ALU.mult, ["gates", "rden"], ["coef"])
                    cb = coef[:, :].unsqueeze(2).to_broadcast([128, 4, 64])
                    oacc = oaccs[(4 * mt + qi) % 2]
                    okk = "oacc%d" % ((4 * mt + qi) % 2)
                    oh = v3(oacc[:, h * 256:(h + 1) * 256], 4)
                    if kind == "win":
                        tt(oh, O3[:, :, 0:64], cb, ALU.mult, [okey, "coef"], [okk])
                    else:
                        tt(v3(tmpO[:, :], 4), O3[:, :, 0:64], cb, ALU.mult, [okey, "coef"], ["tmpO"])
                        tt(oacc[:, h * 256:(h + 1) * 256], oacc[:, h * 256:(h + 1) * 256], tmpO[:, :], ALU.add,
                           [okk, "tmpO"], [okk])
                    if it["qlast"]:
                        for kc in range(8):
                            tr(pT[:, kc * 128:(kc + 1) * 128], oacc[:, kc * 128:(kc + 1) * 128], [okk], ["pb3"])
                        cp(v3(ARENA[:, OT_OFF:OT_OFF + 4096], 8)[:, :, qi * 128:(qi + 1) * 128], v3(pT, 8), ["pb3"], ["oT"])
                deferred.append([1, post])

            NI = len(items)
            pendingB = {}
            deferred = []

            def run_deferred(force=False):
                keep = []
                for d in deferred:
                    d[0] -= 1
                    if force or d[0] < 0:
                        d[1]()
                    else:
                        keep.append(d)
                deferred[:] = keep
            for i in range(min(4, NI)):
                emit_S(items[i])
            if NI:
                emit_M(items[0])
            for i in range(NI):
                if i + 4 < NI:
                    emit_S(items[i + 4])
                if i + 1 < NI:
                    emit_M(items[i + 1])
                emit_E(items[i])
                run_deferred()
                emit_PV(items[i])
                if "fireB" in items[i]:
                    if items[i]["fireB"] not in pendingB:
                        run_deferred(force=True)
                    pendingB.pop(items[i]["fireB"])()
            run_deferred(force=True)

            if stage < 5:
                continue
            for jc in range(8):
                wco_t, wcok = ring_load(wco[jc], 512)
                wgc, wgck = ring_load(wF[20 + jc * 3 + 0])
                wao, waok = ring_load(wF[20 + jc * 3 + 1])
                wga, wgak = ring_load(wF[20 + jc * 3 + 2])
                b1 = proj_F(wco_t, wcok, lambda kc: bcTb[:, kc * 512:(kc + 1) * 512], ["bcT"], K=4)
                b2 = proj_F(wgc, wgck, xnT_kc, ["xnT"])
                act(sg1, pb[b2][:, :], AF.Sigmoid, ["pb%d" % b2], ["sg1"])
                tt(mtmp[:, :], sg1, pb[b1][:, :], ALU.mult, ["sg1", "pb%d" % b1], ["mtmp"])
                b3 = proj_F(wao, waok, lambda kc: ARENA[:, OT_OFF + kc * 512: OT_OFF + (kc + 1) * 512], ["oT"])
                b4 = proj_F(wga, wgak, xnT_kc, ["xnT"])
                act(sg2, pb[b4][:, :], AF.Sigmoid, ["pb%d" % b4], ["sg2"])
                tt(sg2, sg2, pb[b3][:, :], ALU.mult, ["sg2", "pb%d" % b3], ["sg2"])
                tt(ARENA[:, MIX_OFF + jc * 512: MIX_OFF + (jc + 1) * 512], mtmp[:, :], sg2, ALU.add,
                   ["mtmp", "sg2"], ["mixT"])
            wo_t = [ring_load(w_o[kc * 128:(kc + 1) * 128, :]) for kc in range(8)]
            for s in range(4):
                for hf in range(2):
                    b = next_bank()
                    for kc in range(8):
                        mm(pb[b][:, :], ARENA[:, MIX_OFF + kc * 512 + s * 128: MIX_OFF + kc * 512 + (s + 1) * 128],
                           wo_t[kc][0][:, hf * 512:(hf + 1) * 512], kc == 0, kc == 7, ["mixT", wo_t[kc][1]], ["pb%d" % b])
                    xs = xt[:, s * 1024 + hf * 512: s * 1024 + (hf + 1) * 512]
                    tt(xs, xs, pb[b][:, :], ALU.add, ["xt%d" % s, "pb%d" % b], ["xt%d" % s])
                if s >= 1:
                    xn_p2(s - 1, gffnT, "gffnT")
                xn_p1(s)
            xn_p2(3, gffnT, "gffnT")
            if stage < 6:
                continue
            for f in range(NF):
                wg, wgk = ring_load(wF[44 + f * 2 + 0])
                wu, wuk = ring_load(wF[44 + f * 2 + 1])
                bg = proj_F(wg, wgk, xnT_kc, ["xnT"])
                bu = proj_F(wu, wuk, xnT_kc, ["xnT"])
                act(sil, pb[bg][:, :], AF.Silu, ["pb%d" % bg], ["sil"])
                tt(ARENA[:, ACT_OFF + f * 512: ACT_OFF + (f + 1) * 512], sil, pb[bu][:, :], ALU.mult,
                   ["sil", "pb%d" % bu], ["actT%d" % f])
            for f in range(NF):
                wd, wdk = ring_load(w_down[f * 128:(f + 1) * 128, :])
                for s in range(4):
                    for hf in range(2):
                        bi = s * 2 + hf
                        mm(pb[bi][:, :], ARENA[:, ACT_OFF + f * 512 + s * 128: ACT_OFF + f * 512 + (s + 1) * 128],
                           wd[:, hf * 512:(hf + 1) * 512], f == 0, f == NF - 1, ["actT%d" % f, wdk], ["pb%d" % bi])
            for s in range(4):
                for hf in range(2):
                    bi = s * 2 + hf
                    xs = xt[:, s * 1024 + hf * 512: s * 1024 + (hf + 1) * 512]
                    tt(xs, xs, pb[bi][:, :], ALU.add, ["xt%d" % s, "pb%d" % bi], ["xt%d" % s])
                xs = xt[:, s * 1024:(s + 1) * 1024]
                memset(ss[:, s:s + 1], 0.0, "ss", eng="dve")
                act(outb[:, :], xs, AF.Square, ["xt%d" % s], ["outb", "ss"], accum_out=ss[:, s:s + 1])
                act(rstd[:, s:s + 1], ss[:, s:s + 1], AF.Sqrt, ["ss"], ["rstd"], bias=EPS, scale=1.0 / DM)
                S.add("dve", lambda e, s=s: e.reciprocal(out=rstd[:, s:s + 1], in_=rstd[:, s:s + 1]),
                      reads=["rstd"], writes=["rstd"])
                stt(outb[:, :], xs, rstd[:, s:s + 1], gfin[:, :], ALU.mult, ALU.mult, ["xt%d" % s, "rstd", "gfin"], ["outb"])
                r0 = (mt * 4 + s) * 128
                dma_sp(out[r0:r0 + 128, :], outb[:, :], "outst", reads=["outb"])

        if dbg is not None:
            name, shape, apfn = dbg
            S.barrier(["sp"])
            dma_sp(dbg_out, apfn(locals()), "dbgst")
        S.emit(nc)
    return nc


def _tileF(w):
    k = w.shape[0] // 128
    return np.ascontiguousarray(w.reshape(k, 128, w.shape[1]).transpose(1, 0, 2).reshape(128, k * w.shape[1]))


def _host_consts():
    ident = np.eye(128, dtype=np.float32)
    eneg = np.zeros((64, 4096), np.float32)
    for jb in range(64):
        eneg[jb, jb * 64:(jb + 1) * 64] = -30000.0
    cs = np.arange(256)[:, None] * 16
    s2 = np.arange(64)[None, :] * 64
    ovm = np.maximum(0, np.minimum(cs + 32, s2 + 64) - np.maximum(cs, s2)).astype(np.float32) / 32.0
    ovm[255] = 0.0
    ov = np.ascontiguousarray(ovm.reshape(2, 128, 64).transpose(1, 0, 2).reshape(128, 128))
    k = np.arange(128)[:, None]
    q = np.arange(128)[None, :]
    tri = (k <= q).astype(np.float32)
    win4 = (k > q).astype(np.float32)
    return ident, eneg, ov, tri, win4


def _core_consts(p):
    t = np.arange(NOWN)
    tg = t + (NOWN if p == 1 else 0)
    tl = t + NOWN
    i = np.arange(256)[:, None]
    end_l = 16 * i + 31
    ok = (end_l <= tl[None, :]) & (i < 255)
    if p == 0:
        ok &= (16 * i >= NOWN)
    cm = ok.astype(np.float32)
    cmpm = np.ascontiguousarray(cm.reshape(2, 128, NOWN).transpose(1, 0, 2).reshape(128, 2 * NOWN))
    jl = np.arange(64)[None, :]
    cur_l = (tl // 64)[:, None]
    first_l = 0 if p == 1 else 32
    bias = np.zeros((NOWN, 64), np.float32)
    bias[np.broadcast_to(jl > cur_l, bias.shape)] = -1.0e30
    if p == 0:
        bias[:, :32] = -1.0e30
    m = (jl == cur_l - 1) & (jl >= first_l)
    bias[np.broadcast_to(m, bias.shape)] = 1.0e30
    bias[np.broadcast_to(jl == cur_l, bias.shape)] = 2.0e30
    bias[:, first_l] = 3.0e30
    biasd = np.ascontiguousarray(bias.reshape(16, 128, 64).transpose(1, 0, 2).reshape(128, 1024))
    valid = np.ones(NCTX, np.float32)
    if p == 0:
        valid[:NOWN] = 0.0
    valid = np.ascontiguousarray(valid.reshape(32, 128).T)
    return cmpm, biasd, valid


def _prep_shared(w_in, conv_w, w_conv_out, cmp_pos_k, cmp_w1_k, cmp_w2_k, cmp_pos_v, cmp_w1_v, cmp_w2_v,
                 w_attn_out, w_o, g_mix, g_ffn, w_gate, w_up, w_down, g_final):
    f = lambda a: np.asarray(a, dtype=np.float32)
    w_in = f(w_in)[0]
    oB, oC, oH, oQ, oKC, oVC, oKS, oVS, oKW, oVW, oBR, oGC, oGA = (
        0, 512, 1024, 1536, 2560, 2816, 3072, 3328, 3584, 3840, 4096, 4144, 5168)
    cols = lambda o, n: w_in[:, o:o + n]
    kvF = []
    for o in (oKS, oKW, oKC, oVC):
        for c in range(2):
            kvF.append(_tileF(cols(o + c * 128, 128)))
    wkvF = np.stack(kvF)
    wkvT = _tileF(np.concatenate([cols(oVS, 256), cols(oVW, 256)], axis=1))
    tiles = []
    for pair in range(2):
        for g in range(4):
            h0 = (2 * pair) * 4 + g
            h1 = (2 * pair + 1) * 4 + g
            tiles.append(_tileF(np.concatenate([cols(oQ + h0 * 64, 64), cols(oQ + h1 * 64, 64)], axis=1)))
    for cc in range(4):
        tiles.append(_tileF(cols(oC + cc * 128, 128)))
        tiles.append(_tileF(cols(oH + cc * 128, 128)))
        tiles.append(_tileF(cols(oB + cc * 128, 128)))
    wao = f(w_attn_out)[0]
    for jc in range(8):
        tiles.append(_tileF(cols(oGC + jc * 128, 128)))
        tiles.append(_tileF(wao[:, jc * 128:(jc + 1) * 128]))
        tiles.append(_tileF(cols(oGA + jc * 128, 128)))
    wg = f(w_gate)[0]
    wu = f(w_up)[0]
    for ff in range(NF):
        tiles.append(_tileF(wg[:, ff * 128:(ff + 1) * 128]))
        tiles.append(_tileF(wu[:, ff * 128:(ff + 1) * 128]))
    wF = np.stack(tiles)
    assert wF.shape[0] == 88
    wbr = _tileF(cols(oBR, 48))
    wc = f(w_conv_out)[0]
    wco = np.stack([_tileF(wc[:, jc * 128:(jc + 1) * 128]) for jc in range(8)])
    w1 = lambda w: np.ascontiguousarray(f(w)[0].reshape(32, 64, 256).transpose(1, 0, 2).reshape(64, 8192))
    w2k = f(cmp_w2_k)[0].reshape(2, 128, 64).transpose(1, 0, 2)
    w2k = np.ascontiguousarray(np.concatenate([w2k, w2k], axis=2).reshape(128, 256))
    w2v = np.ascontiguousarray(f(cmp_w2_v)[0].reshape(2, 128, 64).transpose(1, 0, 2).reshape(128, 128))
    ident, eneg, ov, tri, win4 = _host_consts()
    vecT = lambda g: np.ascontiguousarray(f(g).reshape(8, 128).T)
    return {
        "wkvF": wkvF, "wkvT": wkvT, "wF": wF, "wbr": wbr, "wco": wco,
        "w_o": np.ascontiguousarray(f(w_o)[0]), "w_down": np.ascontiguousarray(f(w_down)[0]),
        "w1k": w1(cmp_w1_k), "w1v": w1(cmp_w1_v), "w2k": w2k, "w2v": w2v,
        "posk": np.ascontiguousarray(f(cmp_pos_k)[0].T), "posv": np.ascontiguousarray(f(cmp_pos_v)[0].T),
        "convw": np.ascontiguousarray(f(conv_w)[0].reshape(3, 4, 128).transpose(2, 1, 0).reshape(128, 12)),
        "gmixT": vecT(g_mix), "gffnT": vecT(g_ffn), "gfin": np.ascontiguousarray(f(g_final).reshape(1, 1024)),
        "ident": ident, "eneg": eneg, "ov": ov, "tri": tri, "win4": win4,
    }


def make_in_maps(x, **weights):
    x = np.asarray(x, dtype=np.float32)
    shared = _prep_shared(**weights)
    cc = [_core_consts(0), _core_consts(1)]
    in_maps = []
    for c in range(NCORES):
        b, p = c // 2, c % 2
        if p == 1:
            xcz = np.ascontiguousarray(x[b])
        else:
            xcz = np.concatenate([np.zeros((NOWN, DM), np.float32), x[b, :NOWN]], axis=0)
        cmpm, biasd, valid = cc[p]
        d = dict(shared)
        d.update({"xc": xcz, "cmpm": cmpm, "biasd": biasd, "valid": valid})
        in_maps.append(d)
    return in_maps


_PROG = {}


def kernel(x, w_in, conv_w, w_conv_out, cmp_pos_k, cmp_w1_k, cmp_w2_k, cmp_pos_v, cmp_w1_v, cmp_w2_v,
           w_attn_out, w_o, g_mix, g_ffn, w_gate, w_up, w_down, g_final):
    in_maps = make_in_maps(x, w_in=w_in, conv_w=conv_w, w_conv_out=w_conv_out, cmp_pos_k=cmp_pos_k,
                           cmp_w1_k=cmp_w1_k, cmp_w2_k=cmp_w2_k, cmp_pos_v=cmp_pos_v, cmp_w1_v=cmp_w1_v,
                           cmp_w2_v=cmp_w2_v, w_attn_out=w_attn_out, w_o=w_o, g_mix=g_mix, g_ffn=g_ffn,
                           w_gate=w_gate, w_up=w_up, w_down=w_down, g_final=g_final)
    if "nc" not in _PROG:
        _PROG["nc"] = build_program()
    res = run_bass_kernel_spmd(_PROG["nc"], in_maps, core_ids=list(range(NCORES)))
    outf = np.empty((4, 4096, DM), np.float32)
    for c in range(NCORES):
        b, p = c // 2, c % 2
        outf[b, p * NOWN:(p + 1) * NOWN] = np.asarray(res.results[c]["out"], dtype=np.float32)
    return outf
```

```python
import os
import numpy as np
from contextlib import ExitStack
import concourse.bass as bass
import concourse.mybir as mybir
from concourse.bass_utils import run_bass_kernel_spmd

F32 = mybir.dt.float32
BF16 = mybir.dt.bfloat16
AF = mybir.ActivationFunctionType
ALU = mybir.AluOpType

NCORES = 8
DM = 1024
NCTX = 4096
NOWN = 2048
T = 512
DFF = 2816
NF = DFF // 128
NRING = 12
EPS = 1e-6
KSTOP = int(os.environ.get('KSTOP', '99'))


class Sched:
    ENGS = ("pe", "act", "dve", "pool", "sp")

    def __init__(self):
        self.ops = {e: [] for e in self.ENGS}
        self.count = {e: 0 for e in self.ENGS}
        self.dma_count = {}
        self.last_w = {}
        self.readers = {}
        self.seen = {e: {} for e in self.ENGS}

    def add(self, eng, fn, reads=(), writes=(), dma=None):
        waits = {}
        seen = self.seen[eng]

        def need(tok):
            k, v = tok
            if k == eng and eng == "pe":
                return
            if seen.get(k, 0) >= v:
                return
            if waits.get(k, 0) < v:
                waits[k] = v

        for r in reads:
            t = self.last_w.get(r)
            if t is not None:
                need(t)
        for w in writes:
            t = self.last_w.get(w)
            if t is not None:
                need(t)
            for t in self.readers.get(w, ()):
                need(t)
        for k, v in waits.items():
            seen[k] = v
        if dma is not None:
            n = self.dma_count.get(dma, 0) + 1
            self.dma_count[dma] = n
            tok = (dma, 16 * n)
        else:
            self.count[eng] += 1
            tok = (eng, self.count[eng])
        for r in reads:
            self.readers.setdefault(r, []).append(tok)
        for w in writes:
            self.last_w[w] = tok
            self.readers[w] = []
        self.ops[eng].append((tuple(waits.items()), fn, tok))
        return tok

    def barrier(self, engs):
        cur = {}
        for k in self.ENGS:
            if self.count[k]:
                cur[k] = self.count[k]
        for k, n in self.dma_count.items():
            cur[k] = 16 * n
        for e in engs:
            seen = self.seen[e]
            waits = {}
            for k, v in cur.items():
                if k == e:
                    continue
                if seen.get(k, 0) < v:
                    waits[k] = v
                    seen[k] = v
            if waits:
                self.ops[e].append((tuple(waits.items()), None, None))

    def emit(self, nc, final_eng="sp"):
        keys = list(self.ENGS) + sorted(self.dma_count.keys())
        with ExitStack() as es:
            sems = {}
            for k in keys:
                sems[k] = es.enter_context(nc.semaphore("s_" + str(k)))
            block = es.enter_context(nc.Block())
            final = {}
            for k in keys:
                if k in self.ENGS:
                    if self.count[k]:
                        final[k] = self.count[k]
                else:
                    final[k] = 16 * self.dma_count[k]

            def run(e, name):
                for waits, fn, tok in self.ops[name]:
                    for k, v in waits:
                        e.wait_ge(sems[k], v)
                    if fn is None:
                        continue
                    ins = fn(e)
                    k, v = tok
                    ins.then_inc(sems[k], 1 if k in self.ENGS else 16)
                if name == final_eng:
                    for k, v in final.items():
                        e.wait_ge(sems[k], v)

            @block.sync
            def _(e):
                run(e, "sp")

            @block.tensor
            def _(e):
                run(e, "pe")

            @block.scalar
            def _(e):
                run(e, "act")

            @block.vector
            def _(e):
                run(e, "dve")

            @block.gpsimd
            def _(e):
                run(e, "pool")


def build_program(n_own_mt=4, dbg=None, stage=9, n_kv_mt=8):
    nc = bass.Bass("TRN2", target_bir_lowering=False)
    S = Sched()

    def din(name, shape):
        return nc.dram_tensor(name, list(shape), F32, kind="ExternalInput").ap()

    xc = din("xc", [NCTX, DM])
    wkvF = din("wkvF", [8, 128, 1024])
    wkvT = din("wkvT", [128, 4096])
    wF = din("wF", [88, 128, 1024])
    wbr = din("wbr", [128, 384])
    wco = din("wco", [8, 128, 512])
    w_o = din("w_o", [1024, 1024])
    w_down = din("w_down", [DFF, 1024])
    w1k = din("w1k", [64, 8192])
    w1v = din("w1v", [64, 8192])
    w2k = din("w2k", [128, 256])
    w2v = din("w2v", [128, 128])
    posk = din("posk", [64, 32])
    posv = din("posv", [64, 32])
    convw = din("convw", [128, 12])
    gmixT_d = din("gmixT", [128, 8])
    gffnT_d = din("gffnT", [128, 8])
    gfin_d = din("gfin", [1, 1024])
    ident_d = din("ident", [128, 128])
    eneg_d = din("eneg", [64, 4096])
    ov_d = din("ov", [128, 128])
    tri_d = din("tri", [128, 128])
    win4_d = din("win4", [128, 128])
    cmpm_d = din("cmpm", [128, 4096])
    biasd_d = din("biasd", [128, 1024])
    valid_d = din("valid", [128, 32])
    out = nc.dram_tensor("out", [NOWN, DM], F32, kind="ExternalOutput").ap()
    dbg_out = None
    if dbg is not None:
        dbg_out = nc.dram_tensor("dbg", list(dbg[1]), F32, kind="ExternalOutput").ap()

    with ExitStack() as es:
        def sb(name, shape, dt):
            return es.enter_context(nc.sbuf_tensor(name, list(shape), dt))

        KE = [sb("KE%d" % h, [128, 4096], BF16) for h in range(4)]
        kwZ = [sb("kwZ%d" % h, [128, 2560], BF16) for h in range(4)]
        vsA = sb("vsA", [128, 32 * 260], BF16)
        vwA = sb("vwA", [128, 20 * 260], BF16)
        kcmpZ = sb("kcmpZ", [128, 1024], BF16)
        vcmpA = sb("vcmpA", [128, 2 * 260], BF16)
        ARENA = sb("ARENA", [128, 16384], BF16)
        WKV = sb("WKV", [128, 12288], BF16)
        xt = sb("xt", [128, 4 * 1024], F32)
        xnT = sb("xnT", [128, 8 * 512], BF16)
        xn = sb("xn", [128, 1024], BF16)
        AR2 = sb("AR2", [128, 8192], BF16)
        ss = sb("ss", [128, 4], F32)
        rstd = sb("rstd", [128, 4], F32)
        gates = sb("gates", [128, 4 * 48], F32)
        den = sb("den", [128, 4], F32)
        rden = sb("rden", [128, 4], F32)
        coef = sb("coef", [128, 4], F32)
        impv = sb("impv", [128, 64], F32)
        v2 = sb("v2", [128, 64], F32)
        m1 = sb("m1", [128, 8], F32)
        m2 = sb("m2", [128, 8], F32)
        thr = sb("thr", [128, 1], F32)
        selp = sb("selp", [128, 128], BF16)
        uprev = sb("uprev", [128, 8], F32)
        xnTh = sb("xnTh", [128, 16], BF16)
        cSh = sb("cSh", [128, 2], F32)
        biasT = sb("biasT", [128, 2], F32)
        geT2 = [sb("geT%d" % i, [128, 512], BF16) for i in range(2)]
        FS = sb("FS", [128, 2562], F32)
        cS = FS[:, 0:512]
        cv = FS[:, 512:1024]
        ub = FS[:, 1024:1538]
        mtmp = FS[:, 1538:2050]
        outb = FS[:, 0:1024]
        gx = FS[:, 0:256]
        gy = FS[:, 512:768]
        gs = FS[:, 1024:1280]
        oaccs = [sb("oacc%d" % i, [128, 1024], BF16) for i in range(2)]
        tmpO = sb("tmpO", [128, 256], F32)
        identb = sb("identb", [128, 128], BF16)
        ovb = sb("ovb", [128, 128], BF16)
        trib = sb("trib", [128, 128], BF16)
        win4b = sb("win4b", [128, 128], BF16)
        biasd = sb("biasd_s", [128, 256], F32)
        validb = sb("validb", [128, 32], F32)
        convwb = sb("convwb", [128, 12], F32)
        gmixT = sb("gmixT_s", [128, 8], F32)
        gffnT = sb("gffnT_s", [128, 8], F32)
        gfin = sb("gfin_s", [128, 1024], F32)
        wbrb = sb("wbrb", [128, 384], BF16)
        w2kb = sb("w2kb", [128, 256], BF16)
        w2vb = sb("w2vb", [128, 128], BF16)
        poskb = sb("poskb", [64, 32], BF16)
        posvb = sb("posvb", [64, 32], BF16)

        pb = [es.enter_context(nc.psum_tensor("pb%d" % i, [128, 512], F32)) for i in range(8)]
        pT = pb[3][:, :].bitcast(BF16)

        def dma_cast(dst, src, key, n, extra_w=()):
            if n <= 2048:
                S.add("pool", lambda e: e.dma_start(out=dst, in_=src), writes=[key] + list(extra_w), dma=key)
            else:
                a = n // 2048
                d3 = dst.rearrange("p (a b) -> p a b", a=a)
                s3 = src.rearrange("p (a b) -> p a b", a=a)
                S.add("pool", lambda e: e.dma_start(out=d3, in_=s3), writes=[key] + list(extra_w), dma=key)

        def dma_sp(dst, src, key, reads=(), writes=()):
            S.add("sp", lambda e: e.dma_start(out=dst, in_=src), reads=list(reads), writes=list(writes), dma=key)

        def mm(o, lhsT, rhs, start, stop, reads, writes, sgc=False):
            S.add("pe", lambda e: e.matmul(o, lhsT=lhsT, rhs=rhs, start=start, stop=stop, skip_group_check=sgc),
                  reads=reads, writes=writes)

        def tr(o, i, reads, writes):
            S.add("pe", lambda e: e.transpose(out=o, in_=i, identity=identb[0:i.shape[0], 0:i.shape[0]]),
                  reads=list(reads) + ["identb"], writes=writes)

        def act(o, i, func, reads, writes, **kw):
            S.add("act", lambda e: e.activation(out=o, in_=i, func=func, **kw), reads=reads, writes=writes)

        def tt(o, a, b, op, reads, writes, eng="dve"):
            S.add(eng, lambda e: e.tensor_tensor(out=o, in0=a, in1=b, op=op), reads=reads, writes=writes)

        def ts(o, a, s1, s2, op0, op1, reads, writes, eng="dve"):
            if s2 is None:
                S.add(eng, lambda e: e.tensor_scalar(out=o, in0=a, scalar1=s1, scalar2=None, op0=op0),
                      reads=reads, writes=writes)
            else:
                S.add(eng, lambda e: e.tensor_scalar(out=o, in0=a, scalar1=s1, scalar2=s2, op0=op0, op1=op1),
                      reads=reads, writes=writes)

        def stt(o, a, sc, b, op0, op1, reads, writes):
            S.add("dve", lambda e: e.scalar_tensor_tensor(out=o, in0=a, scalar=sc, in1=b, op0=op0, op1=op1),
                  reads=reads, writes=writes)

        def cp(o, i, reads, writes, eng="dve"):
            if eng == "act":
                S.add("act", lambda e: e.copy(out=o, in_=i), reads=reads, writes=writes)
            else:
                S.add(eng, lambda e: e.tensor_copy(out=o, in_=i), reads=reads, writes=writes)

        def memset(ap, val, key, eng="pool"):
            S.add(eng, lambda e: e.memset(ap, val), writes=[key])

        bank_ctr = [0]

        def next_bank():
            b = (0, 1, 2, 6, 7)[bank_ctr[0] % 5]
            bank_ctr[0] += 1
            return b

        evac_ctr = [0]

        def evac_eng():
            evac_ctr[0] += 1
            return "act" if evac_ctr[0] % 2 else "dve"

        def v3(ap, a):
            return ap.rearrange("p (a b) -> p a b", a=a)

        dma_cast(identb[:, :], ident_d, "identb", 128)
        dma_sp(gmixT[:, :], gmixT_d, "gmixT", writes=["gmixT"])
        for j in range(8):
            dma_cast(WKV[:, j * 1024:(j + 1) * 1024], wkvF[j], "wkvF%d" % j, 1024)
        dma_cast(WKV[:, 8192:12288], wkvT, "wkvT", 4096)
        dma_sp(validb[:, :], valid_d, "validb", writes=["validb"])
        for h in range(4):
            memset(kwZ[h][:, :], 0.0, "kwT")
        memset(kcmpZ[:, :], 0.0, "kcmpT")
        memset(vcmpA[:, :], 1.0, "vcmpA")
        memset(geT2[0][:, :], 0.0, "geT0")
        memset(geT2[1][:, :], 0.0, "geT1")
        memset(uprev[:, :], 0.0, "uprev")
        memset(selp[:, :], 0.0, "selp")
        dma_cast(w2kb[:, :], w2k, "w2kb", 256)
        dma_cast(w2vb[:, :], w2v, "w2vb", 128)
        dma_cast(poskb[:, :], posk, "poskb", 32)
        dma_cast(posvb[:, :], posv, "posvb", 32)
        for h in range(4):
            oh_ = 1 - (h % 2)
            dma_cast(KE[h][64 * oh_:64 * oh_ + 64, :], eneg_d, "KEe%d" % h, 4096)
        dma_cast(ovb[:, :], ov_d, "ovb", 128)
        dma_cast(trib[:, :], tri_d, "trib", 128)
        dma_cast(win4b[:, :], win4_d, "win4b", 128)
        dma_cast(wbrb[:, :], wbr, "wbrb", 384)
        dma_sp(convwb[:, :], convw, "convwb", writes=["convwb"])
        dma_sp(gffnT[:, :], gffnT_d, "gffnT", writes=["gffnT"])
        dma_sp(gfin[:, :], gfin_d.partition_broadcast(128), "gfin", writes=["gfin"])
        vs4 = vsA[:, :].rearrange("p (c h d) -> p c h d", c=32, h=4)
        vw4 = vwA[:, :].rearrange("p (c h d) -> p c h d", c=20, h=4)
        for h in range(4):
            cp(vs4[:, :, h, 64], validb[:, :], ["validb"], ["vsA"])
            cp(vw4[:, :, h, 64], validb[:, 12:32], ["validb"], ["vwA"])

        def xn_p1(s, xb=None, xk="xn"):
            xs = xt[:, s * 1024:(s + 1) * 1024]
            if xb is None:
                xb = xn[:, :]
            memset(ss[:, s:s + 1], 0.0, "ss", eng="dve")
            act(FS[:, 0:1024], xs, AF.Square, ["xt%d" % s], ["sqjunk", "ss"], accum_out=ss[:, s:s + 1])
            act(rstd[:, s:s + 1], ss[:, s:s + 1], AF.Sqrt, ["ss"], ["rstd"], bias=EPS, scale=1.0 / DM)
            S.add("dve", lambda e, s=s: e.reciprocal(out=rstd[:, s:s + 1], in_=rstd[:, s:s + 1]),
                  reads=["rstd"], writes=["rstd"])
            ts(xb, xs, rstd[:, s:s + 1], None, ALU.mult, None, ["xt%d" % s, "rstd"], [xk])

        def xn_p2(s, gT, gkey, dstT=None, dkey="xnT", xb=None, xk="xn"):
            if dstT is None:
                dstT = xnT[:, :]
            if xb is None:
                xb = xn[:, :]
            for kc in range(8):
                tr(pT[:, kc * 128:(kc + 1) * 128], xb[:, kc * 128:(kc + 1) * 128], [xk], ["pb3"])
            tt(v3(dstT, 8)[:, :, s * 128:(s + 1) * 128], v3(pT, 8),
               gT[:, :].unsqueeze(2).to_broadcast([128, 8, 128]), ALU.mult, ["pb3", gkey], [dkey])

        def xn_sub(s, gT, gkey, dstT=None, dkey="xnT"):
            xn_p1(s)
            xn_p2(s, gT, gkey, dstT, dkey)

        def make_xnT(gT, gkey):
            for s in range(4):
                xn_sub(s, gT, gkey)

        def proj_F(wtile, wkey, rhs_of, rkeys, K=8):
            b = next_bank()
            for kc in range(K):
                mm(pb[b][:, :], wtile[:, kc * 128:(kc + 1) * 128], rhs_of(kc), kc == 0, kc == K - 1,
                   [wkey] + rkeys, ["pb%d" % b])
            return b

        xnT_kc = lambda kc: xnT[:, kc * 512:(kc + 1) * 512]

        kvdst = [(None, 4096, 0, 0), (None, 4096, 1, 0), ("kw", 2560, 0, 1536), ("kw", 2560, 1, 1536),
                 (ARENA, 4096, 0, 0), (ARENA, 4096, 1, 0), (ARENA, 4096, 2, 0), (ARENA, 4096, 3, 0)]
        kvkeys = ["ksT", "ksT", "kwT", "kwT", "kcT", "kcT", "vcT", "vcT"]
        xnT_bufs = [(AR2[:, 0:4096], "xnTB"), (xnT[:, :], "xnT")]

        def kv_chain_p1(m, s_):
            dma_sp(xt[:, s_ * 1024:(s_ + 1) * 1024], xc[m * 512 + s_ * 128: m * 512 + (s_ + 1) * 128, :],
                   "xt%d" % s_, writes=["xt%d" % s_])
            xn_p1(s_)

        def kv_chain_p2(m, s_):
            dstT, dkey = xnT_bufs[m % 2]
            xn_p2(s_, gmixT, "gmixT", dstT, dkey)

        def kv_chain_sub(m, s_):
            kv_chain_p1(m, s_)
            kv_chain_p2(m, s_)

        def kv_F(m, j):
            xb, xk = xnT_bufs[m % 2]
            buf, width, ci, t0 = kvdst[j]
            if m * 512 < t0:
                return
            b = proj_F(WKV[:, j * 1024:(j + 1) * 1024], "wkvF%d" % j, lambda kc: xb[:, kc * 512:(kc + 1) * 512], [xk])
            col = ci * width + m * 512 - t0
            if buf is None:
                ee = evac_eng()
                cp(KE[2 * ci][0:64, m * 512:(m + 1) * 512], pb[b][0:64, :], ["pb%d" % b], ["ksT"], eng=ee)
                cp(KE[2 * ci + 1][64:128, m * 512:(m + 1) * 512], pb[b][64:128, :], ["pb%d" % b], ["ksT"], eng=ee)
            elif buf == "kw":
                ee = evac_eng()
                c0 = m * 512 - t0
                cp(kwZ[2 * ci][0:64, c0:c0 + 512], pb[b][0:64, :], ["pb%d" % b], ["kwT"], eng=ee)
                cp(kwZ[2 * ci + 1][64:128, c0:c0 + 512], pb[b][64:128, :], ["pb%d" % b], ["kwT"], eng=ee)
            else:
                dst = ARENA[:, ci * 4096:(ci + 1) * 4096].rearrange("p (r i) -> p r i", r=16)[:, :, m * 32:(m + 1) * 32]
                src = pb[b][:, :].rearrange("p (i r) -> p r i", r=16)
                cp(dst, src, ["pb%d" % b], [kvkeys[j]], eng=evac_eng())

        def kv_T(m, s):
            xb, xk = xnT_bufs[m % 2]
            b = next_bank()
            for kc in range(8):
                mm(pb[b][:, :], xb[:, kc * 512 + s * 128: kc * 512 + (s + 1) * 128],
                   WKV[:, 8192 + kc * 512: 8192 + (kc + 1) * 512], kc == 0, kc == 7,
                   [xk, "wkvT"], ["pb%d" % b])
            c = 4 * m + s
            ee = evac_eng()
            cp(vs4[:, c, :, 0:64], v3(pb[b][:, 0:256], 4), ["pb%d" % b], ["vsA"], eng=ee)
            if c >= 12:
                cp(vw4[:, c - 12, :, 0:64], v3(pb[b][:, 256:512], 4), ["pb%d" % b], ["vwA"], eng=ee)

        for s_ in range(4 if n_kv_mt else 0):
            kv_chain_sub(0, s_)
        for m in range(n_kv_mt):
            if m == 3:
                xb, xk = xnT_bufs[m % 2]
                cp(v3(xnTh[:, :], 8), v3(xb, 8)[:, :, 510:512], [xk], ["xnTh"])
            work = [("F", j) for j in range(8)] + [("T", s_) for s_ in range(4)]
            for qtr in range(4):
                if m + 1 < n_kv_mt:
                    kv_chain_p1(m + 1, qtr)
                for kind_, idx_ in work[qtr * 3:(qtr + 1) * 3]:
                    if kind_ == "F":
                        kv_F(m, idx_)
                    else:
                        kv_T(m, idx_)
                if m + 1 < n_kv_mt:
                    kv_chain_p2(m + 1, qtr)

        vc4 = vcmpA[:, :].rearrange("p (c h d) -> p c h d", c=2, h=4)
        for kv in range(2 if stage >= 2 else 0):
            w1src = w1k if kv == 0 else w1v
            posb = poskb if kv == 0 else posvb
            poskey = "poskb" if kv == 0 else "posvb"
            srcoff = 0 if kv == 0 else 8192
            srckey = "kcT" if kv == 0 else "vcT"
            W1 = AR2 if kv == 0 else WKV
            w1ka, w1kb = ("w1a", "w1b") if kv == 0 else ("w1c", "w1d")
            xw = ["xnTB"] if kv == 0 else ["wkvF%d" % j_ for j_ in range(8)]
            dma_cast(W1[0:64, 0:8192], w1src, w1ka, 8192, extra_w=xw)
            dma_cast(W1[64:128, 0:8192], w1src, w1kb, 8192, extra_w=xw)
            for hc in range(2):
                b = next_bank()
                for l in range(32):
                    mm(pb[b][:, 0:1], W1[0:64, l * 256 + hc * 128: l * 256 + (hc + 1) * 128], posb[:, l:l + 1],
                       l == 0, l == 31, [w1ka, poskey], ["pb%d" % b])
                cp(biasT[:, hc:hc + 1], pb[b][:, 0:1], ["pb%d" % b], ["biasT"])
            for pair in range(2):
                hbanks = {}
                for hc in range(2):
                    bs = [next_bank(), next_bank()]
                    for l in range(32):
                        for half in range(2):
                            prt = slice(64 * half, 64 * half + 64)
                            w1key = w1ka if half == 0 else w1kb
                            base = srcoff + pair * 4096 + (l % 16) * 256 + (l // 16)
                            rhs = ARENA[prt, base: base + 255]
                            mm(pb[bs[half]][:, 0:255], W1[prt, l * 256 + hc * 128: l * 256 + (hc + 1) * 128], rhs,
                               l == 0, l == 31, [w1key, srckey], ["pb%d" % bs[half]])
                    hbanks[hc] = bs
                    G = [(FS[:, 0:256], FS[:, 512:768], FS[:, 1024:1280], "gx0", "gy0", "gs0"),
                         (FS[:, 256:512], FS[:, 768:1024], FS[:, 1280:1536], "gx1", "gy1", "gs1")]
                    for half in range(2):
                        gx_, gy_, gs_, kx, ky, ks_ = G[half]
                        b = bs[half]
                        ts(gx_[:, 0:255], pb[b][:, 0:255], biasT[:, hc:hc + 1], None, ALU.add, None,
                           ["pb%d" % b, "biasT"], [kx])
                    for half in range(2):
                        gx_, gy_, gs_, kx, ky, ks_ = G[half]
                        tt(gy_[:, 0:255], gx_[:, 0:255], gx_[:, 0:255], ALU.mult, [kx], [ky])
                    for half in range(2):
                        gx_, gy_, gs_, kx, ky, ks_ = G[half]
                        ts(gy_[:, 0:255], gy_[:, 0:255], 0.044715, 1.0, ALU.mult, ALU.add, [ky], [ky])
                    for half in range(2):
                        gx_, gy_, gs_, kx, ky, ks_ = G[half]
                        tt(gy_[:, 0:255], gy_[:, 0:255], gx_[:, 0:255], ALU.mult, [ky, kx], [ky])
                    for half in range(2):
                        gx_, gy_, gs_, kx, ky, ks_ = G[half]
                        act(gs_[:, 0:255], gy_[:, 0:255], AF.Sigmoid, [ky], [ks_], scale=1.5957691216057308)
                    for half in range(2):
                        gx_, gy_, gs_, kx, ky, ks_ = G[half]
                        tt(geT2[half][:, hc * 256: hc * 256 + 255], gx_[:, 0:255], gs_[:, 0:255], ALU.mult,
                           [kx, ks_], ["geT%d" % half])
                for half in range(2):
                    h = 2 * pair + half
                    prt = slice(64 * half, 64 * half + 64)
                    geTh = geT2[half]
                    gkey = "geT%d" % half
                    if kv == 0:
                        b = next_bank()
                        for hc in range(2):
                            mm(pb[b][:, 0:256], w2kb[:, hc * 128:(hc + 1) * 128], geTh[:, hc * 256:(hc + 1) * 256],
                               hc == 0, hc == 1, ["w2kb", gkey], ["pb%d" % b])
                        cp(kcmpZ[prt, h * 256:(h + 1) * 256], pb[b][prt, 0:256], ["pb%d" % b], ["kcmpT"])
                    else:
                        for bc in range(2):
                            b = next_bank()
                            for hc in range(2):
                                mm(pb[b][:, 0:64], geTh[:, hc * 256 + bc * 128: hc * 256 + (bc + 1) * 128],
                                   w2vb[:, hc * 64:(hc + 1) * 64], hc == 0, hc == 1, ["w2vb", gkey], ["pb%d" % b])
                            cp(vc4[:, bc, h, 0:64], pb[b][:, 0:64], ["pb%d" % b], ["vcmpA"])

        S.barrier(["pe", "act", "dve", "pool"])

        QT_OFF, OT_OFF, MIX_OFF, ACT_OFF = 0, 8192, 12288, 0
        PT = [AR2[:, i * 512:(i + 1) * 512] for i in range(3)] + [AR2[:, 4096:4608], AR2[:, 4608:5120]]
        sg1 = AR2[:, 1536:2048]
        sg2 = AR2[:, 2048:2560]
        sil = AR2[:, 2560:3072]
        cmpm_t = AR2[:, 3072:4096]
        obf = AR2[:, 4096:5120]
        bcTb = AR2[:, 5120:7168]
        ring = [WKV[:, i * 1024:(i + 1) * 1024] for i in range(NRING)]
        ring_ctr = [0]

        def ring_load(src, n=1024):
            i = ring_ctr[0] % NRING
            ring_ctr[0] += 1
            dma_cast(ring[i][:, 0:n], src, "ring%d" % i, n)
            return ring[i], "ring%d" % i

        pt_ctr = [0]
        ob_ctr = [0]
        pm_ctr = [0]

        for mt in range(n_own_mt if stage >= 3 else 0):
            if mt > 0:
                S.barrier(["pe", "act", "dve"])
            m = 4 + mt
            for s_ in range(4):
                dma_sp(xt[:, s_ * 1024:(s_ + 1) * 1024], xc[m * 512 + s_ * 128: m * 512 + (s_ + 1) * 128, :],
                       "xt%d" % s_, writes=["xt%d" % s_])
            dma_cast(v3(cmpm_t, 2), v3(cmpm_d, 2)[:, :, mt * 512:(mt + 1) * 512], "cmpm_t", 512)
            dma_sp(biasd[:, :], biasd_d[:, mt * 256:(mt + 1) * 256], "biasd", writes=["biasd"])
            XB = [(xn[:, :], "xn"), (AR2[:, 7168:8192], "xn2")]
            xn_p1(0, *XB[0])
            for s_ in range(4):
                if s_ + 1 < 4:
                    xn_p1(s_ + 1, *XB[(s_ + 1) % 2])
                xn_p2(s_, gmixT, "gmixT", None, "xnT", *XB[s_ % 2])
            for ci in range(8):
                wt, wk = ring_load(wF[ci])
                b = proj_F(wt, wk, xnT_kc, ["xnT"])
                pair, g = ci // 4, ci % 4
                ee = evac_eng()
                for hh in range(2):
                    h_ = 2 * pair + hh
                    pr_ = slice(64 * hh, 64 * hh + 64)
                    dst = ARENA[pr_, QT_OFF + h_ * 2048: QT_OFF + (h_ + 1) * 2048].rearrange(
                        "p (q g t) -> p q g t", q=4, g=4)[:, :, g, :]
                    cp(dst, v3(pb[b][pr_, :], 4), ["pb%d" % b], ["QT"], eng=ee)
            for cc in range(4):
                wc, wck = ring_load(wF[8 + cc * 3 + 0])
                wh, whk = ring_load(wF[8 + cc * 3 + 1])
                wb_, wbk = ring_load(wF[8 + cc * 3 + 2])
                bc_ = proj_F(wc, wck, xnT_kc, ["xnT"])
                if mt == 0:
                    for kc in range(8):
                        mm(pb[3][:, 256:258], wc[:, kc * 128:(kc + 1) * 128], xnTh[:, kc * 2:(kc + 1) * 2],
                           kc == 0, kc == 7, [wck, "xnTh"], ["pb3"])
                    cp(cSh[:, :], pb[3][:, 256:258], ["pb3"], ["cSh"], eng="act")
                cp(cS[:, :], pb[bc_][:, :], ["pb%d" % bc_], ["cS"], eng="act")
                bh_ = proj_F(wh, whk, xnT_kc, ["xnT"])
                if mt == 0:
                    for kc in range(8):
                        mm(pb[3][:, 264:266], wh[:, kc * 128:(kc + 1) * 128], xnTh[:, kc * 2:(kc + 1) * 2],
                           kc == 0, kc == 7, [whk, "xnTh"], ["pb3"])
                    tt(ub[:, 0:2], cSh[:, :], pb[3][:, 264:266], ALU.mult, ["cSh", "pb3"], ["ub"])
                else:
                    cp(ub[:, 0:2], uprev[:, cc * 2:(cc + 1) * 2], ["uprev"], ["ub"])
                tt(ub[:, 2:514], cS[:, :], pb[bh_][:, :], ALU.mult, ["cS", "pb%d" % bh_], ["ub"])
                cp(uprev[:, cc * 2:(cc + 1) * 2], ub[:, 512:514], ["ub"], ["uprev"])
                ts(cv[:, :], ub[:, 0:512], convwb[:, cc * 3 + 0: cc * 3 + 1], None, ALU.mult, None,
                   ["ub", "convwb"], ["cv"])
                stt(cv[:, :], ub[:, 1:513], convwb[:, cc * 3 + 1: cc * 3 + 2], cv[:, :], ALU.mult, ALU.add,
                    ["ub", "cv", "convwb"], ["cv"])
                stt(cv[:, :], ub[:, 2:514], convwb[:, cc * 3 + 2: cc * 3 + 3], cv[:, :], ALU.mult, ALU.add,
                    ["ub", "cv", "convwb"], ["cv"])
                bb_ = proj_F(wb_, wbk, xnT_kc, ["xnT"])
                tt(bcTb[:, cc * 512:(cc + 1) * 512], cv[:, :], pb[bb_][:, :], ALU.mult,
                   ["cv", "pb%d" % bb_], ["bcT"])
            for s in range(4):
                b = next_bank()
                for kc in range(8):
                    mm(pb[b][:, 0:48], xnT[:, kc * 512 + s * 128: kc * 512 + (s + 1) * 128],
                       wbrb[:, kc * 48:(kc + 1) * 48], kc == 0, kc == 7, ["xnT", "wbrb"], ["pb%d" % b])
                act(gates[:, s * 48:(s + 1) * 48], pb[b][:, 0:48], AF.Sigmoid, ["pb%d" % b], ["gates"])

            items = []
            groups = []
            for qi in range(4 if stage >= 4 else 0):
                j = 4 * mt + qi
                Dg = 16 + j
                for h in range(4):
                    groups.append((qi, j, Dg, h))

            def add_branch(gr, br, kind):
                qi, j, Dg, h = gr
                if kind == "cmp":
                    chunks = [0, 1]
                elif kind == "sel":
                    chunks = list(range(0, Dg + 1))
                else:
                    chunks = list(range(Dg - 4, Dg + 1))
                ob = 4 + (ob_ctr[0] % 2)
                ob_ctr[0] += 1
                for c in chunks:
                    items.append(dict(qi=qi, j=j, Dg=Dg, h=h, br=br, kind=kind, c=c, ob=ob,
                                      first=(c == chunks[0]), last=(c == chunks[-1]),
                                      qlast=(h == 3 and kind == "sel" and c == chunks[-1])))

            for gi, gr in enumerate(groups):
                if gi == 0:
                    add_branch(gr, 2, "win")
                    add_branch(gr, 0, "cmp")
                    items[-1]["fireB"] = (gr[0], gr[3])
                if gi + 1 < len(groups):
                    add_branch(groups[gi + 1], 2, "win")
                    add_branch(groups[gi + 1], 0, "cmp")
                    nxt = groups[gi + 1]
                else:
                    nxt = None
                n0 = len(items)
                add_branch(gr, 1, "sel")
                if nxt is not None:
                    items[min(n0 + 10, len(items) - 1)]["fireB"] = (nxt[0], nxt[3])

            def it_setup(it):
                h, c, kind, qi = it["h"], it["c"], it["kind"], it["qi"]
                half, pair = h % 2, h // 2
                prt = slice(64 * half, 64 * half + 64)
                it["prt"] = prt
                qoff = QT_OFF + h * 2048 + qi * 512
                it["rhsQ"] = ARENA[:, qoff: qoff + 512]
                it["qkeys"] = ["QT", "NS%d_%d" % (h, qi)]
                if kind == "cmp":
                    it["lhsT"] = kcmpZ[:, h * 256 + c * 128: h * 256 + (c + 1) * 128]
                    it["kkey"] = "kcmpT"
                    it["vaug"] = vc4[:, c, h, :]
                    it["vkey"] = "vcmpA"
                elif kind == "sel":
                    it["lhsT"] = KE[h][:, c * 128:(c + 1) * 128]
                    it["qkeys"] = ["QT", "NS%d_%d" % (h, qi), "KEe%d" % h]
                    it["kkey"] = "ksT"
                    it["vaug"] = vs4[:, c, h, :]
                    it["vkey"] = "vsA"
                else:
                    it["lhsT"] = kwZ[h][:, (c - 12) * 128: (c - 11) * 128]
                    it["kkey"] = "kwT"
                    it["vaug"] = vw4[:, c - 12, h, :]
                    it["vkey"] = "vwA"

            def emit_S(it):
                it_setup(it)
                b = next_bank()
                it["b"] = b
                mm(pb[b][:, :], it["lhsT"], it["rhsQ"], True, True, [it["kkey"]] + it["qkeys"], ["pb%d" % b])

            def emit_M(it):
                it["mask"] = None
                kind, c, Dg, qi = it["kind"], it["c"], it["Dg"], it["qi"]
                if kind == "cmp":
                    it["mask"] = cmpm_t[:, c * 512 + qi * 128: c * 512 + (qi + 1) * 128]
                    it["mkey"] = "cmpm_t"
                elif kind == "sel":
                    if c == Dg:
                        it["mask"], it["mkey"] = trib[:, :], "trib"
                else:
                    o = Dg - c
                    if o == 0:
                        it["mask"], it["mkey"] = trib[:, :], "trib"
                    elif o == 4:
                        it["mask"], it["mkey"] = win4b[:, :], "win4b"

            def emit_E(it):
                pi = pt_ctr[0] % 5
                pt_ctr[0] += 1
                P = PT[pi]
                pkey = "PT%d" % pi
                it["P"], it["pkey"] = P, pkey
                act(P, pb[it["b"]][:, :], AF.Exp, ["pb%d" % it["b"]], [pkey], scale=0.125)
                if it["mask"] is not None:
                    tt(v3(P, 4), v3(P, 4), it["mask"].unsqueeze(1).to_broadcast([128, 4, 128]), ALU.mult,
                       [pkey, it["mkey"]], [pkey])

            def emit_PV(it):
                kind, c, qi, h, br, j = it["kind"], it["c"], it["qi"], it["h"], it["br"], it["j"]
                ob = it["ob"]
                okey = "pb%d" % ob
                P, pkey = it["P"], it["pkey"]
                first, last = it["first"], it["last"]
                O3 = v3(pb[ob][:, 0:260], 4)
                for g in range(4):
                    mm(pb[ob][:, g * 65:(g + 1) * 65], P[:, g * 128:(g + 1) * 128], it["vaug"],
                       first and g == 0, last and g == 3, [pkey, it["vkey"]], [okey], sgc=True)
                if kind == "cmp":
                    for g in range(4):
                        mm(pb[3][:, 256 + g * 64: 256 + (g + 1) * 64], P[:, g * 128:(g + 1) * 128],
                           ovb[:, c * 64:(c + 1) * 64], first and g == 0, last and g == 3,
                           [pkey, "ovb"], ["pb3"], sgc=True)
                if not last:
                    return

                def post():
                    ts(den[:, :], O3[:, :, 64], 1e-30, None, ALU.max, None, [okey], ["den"])
                    S.add("dve", lambda e: e.reciprocal(out=rden[:, :], in_=den[:, :]), reads=["den"], writes=["rden"])
                    if kind == "cmp":
                        ts(impv[:, :], pb[3][:, 256:320], rden[:, 0:1], None, ALU.mult, None, ["pb3", "rden"], ["impv"])
                        for g in range(1, 4):
                            stt(impv[:, :], pb[3][:, 256 + g * 64: 256 + (g + 1) * 64], rden[:, g:g + 1], impv[:, :],
                                ALU.mult, ALU.add, ["pb3", "rden", "impv"], ["impv"])
                        tt(impv[:, :], impv[:, :], biasd[:, qi * 64:(qi + 1) * 64], ALU.add, ["impv", "biasd"], ["impv"])
                        S.add("dve", lambda e: e.max(out=m1[:, :], in_=impv[:, :]), reads=["impv"], writes=["m1"])
                        S.add("dve", lambda e: e.match_replace(out=v2[:, :], in_to_replace=m1[:, :],
                                                               in_values=impv[:, :], imm_value=-3.0e38),
                              reads=["impv", "m1"], writes=["v2"])
                        S.add("dve", lambda e: e.max(out=m2[:, :], in_=v2[:, :]), reads=["v2"], writes=["m2"])
                        ts(thr[:, :], m2[:, 7:8], -1.0e29, None, ALU.max, None, ["m2"], ["thr"])
                        oh_ = 1 - (h % 2)
                        opr = slice(64 * oh_, 64 * oh_ + 64)
                        ts(selp[:, 64 * oh_:64 * oh_ + 64], impv[:, :], thr[:, 0:1], None, ALU.is_lt, None,
                           ["impv", "thr"], ["selp"])
                        qoff = QT_OFF + h * 2048 + qi * 512

                        def partB(opr=opr, qoff=qoff, h=h, qi=qi):
                            tr(pT[:, 0:128], selp[:, :], ["selp"], ["pb3"])
                            cp(v3(ARENA[opr, qoff: qoff + 512], 4),
                               pT[opr, 0:128].unsqueeze(1).to_broadcast([64, 4, 128]),
                               ["pb3"], ["NS%d_%d" % (h, qi)])
                        pendingB[(qi, h)] = partB
                    gsl = gates[:, qi * 48 + br * 16 + h * 4: qi * 48 + br * 16 + h * 4 + 4]
                    tt(coef[:, :], gsl, rden[:, :], ALU.mult, ["gates", "rden"], ["coef"])
                    cb = coef[:, :].unsqueeze(2).to_broadcast([128, 4, 64])
                    oacc = oaccs[(4 * mt + qi) % 2]
                    okk = "oacc%d" % ((4 * mt + qi) % 2)
                    oh = v3(oacc[:, h * 256:(h + 1) * 256], 4)
                    if kind == "win":
                        tt(oh, O3[:, :, 0:64], cb, ALU.mult, [okey, "coef"], [okk])
                    else:
                        tt(v3(tmpO[:, :], 4), O3[:, :, 0:64], cb, ALU.mult, [okey, "coef"], ["tmpO"])
                        tt(oacc[:, h * 256:(h + 1) * 256], oacc[:, h * 256:(h + 1) * 256], tmpO[:, :], ALU.add,
                           [okk, "tmpO"], [okk])
                    if it["qlast"]:
                        for kc in range(8):
                            tr(pT[:, kc * 128:(kc + 1) * 128], oacc[:, kc * 128:(kc + 1) * 128], [okk], ["pb3"])
                        cp(v3(ARENA[:, OT_OFF:OT_OFF + 4096], 8)[:, :, qi * 128:(qi + 1) * 128], v3(pT, 8), ["pb3"], ["oT"])
                deferred.append([1, post])

            NI = len(items)
            pendingB = {}
            deferred = []

            def run_deferred(force=False):
                keep = []
                for d in deferred:
                    d[0] -= 1
                    if force or d[0] < 0:
                        d[1]()
                    else:
                        keep.append(d)
                deferred[:] = keep
            for i in range(min(4, NI)):
                emit_S(items[i])
            if NI:
                emit_M(items[0])
            for i in range(NI):
                if i + 4 < NI:
                    emit_S(items[i + 4])
                if i + 1 < NI:
                    emit_M(items[i + 1])
                emit_E(items[i])
                run_deferred()
                emit_PV(items[i])
                if "fireB" in items[i]:
                    if items[i]["fireB"] not in pendingB:
                        run_deferred(force=True)
                    pendingB.pop(items[i]["fireB"])()
            run_deferred(force=True)

            if stage < 5:
                continue
            for jc in range(8):
                wco_t, wcok = ring_load(wco[jc], 512)
                wgc, wgck = ring_load(wF[20 + jc * 3 + 0])
                wao, waok = ring_load(wF[20 + jc * 3 + 1])
                wga, wgak = ring_load(wF[20 + jc * 3 + 2])
                b1 = proj_F(wco_t, wcok, lambda kc: bcTb[:, kc * 512:(kc + 1) * 512], ["bcT"], K=4)
                b2 = proj_F(wgc, wgck, xnT_kc, ["xnT"])
                act(sg1, pb[b2][:, :], AF.Sigmoid, ["pb%d" % b2], ["sg1"])
                tt(mtmp[:, :], sg1, pb[b1][:, :], ALU.mult, ["sg1", "pb%d" % b1], ["mtmp"])
                b3 = proj_F(wao, waok, lambda kc: ARENA[:, OT_OFF + kc * 512: OT_OFF + (kc + 1) * 512], ["oT"])
                b4 = proj_F(wga, wgak, xnT_kc, ["xnT"])
                act(sg2, pb[b4][:, :], AF.Sigmoid, ["pb%d" % b4], ["sg2"])
                tt(sg2, sg2, pb[b3][:, :], ALU.mult, ["sg2", "pb%d" % b3], ["sg2"])
                tt(ARENA[:, MIX_OFF + jc * 512: MIX_OFF + (jc + 1) * 512], mtmp[:, :], sg2, ALU.add,
                   ["mtmp", "sg2"], ["mixT"])
            wo_t = [ring_load(w_o[kc * 128:(kc + 1) * 128, :]) for kc in range(8)]
            for s in range(4):
                for hf in range(2):
                    b = next_bank()
                    for kc in range(8):
                        mm(pb[b][:, :], ARENA[:, MIX_OFF + kc * 512 + s * 128: MIX_OFF + kc * 512 + (s + 1) * 128],
                           wo_t[kc][0][:, hf * 512:(hf + 1) * 512], kc == 0, kc == 7, ["mixT", wo_t[kc][1]], ["pb%d" % b])
                    xs = xt[:, s * 1024 + hf * 512: s * 1024 + (hf + 1) * 512]
                    tt(xs, xs, pb[b][:, :], ALU.add, ["xt%d" % s, "pb%d" % b], ["xt%d" % s])
                if s >= 1:
                    xn_p2(s - 1, gffnT, "gffnT")
                xn_p1(s)
            xn_p2(3, gffnT, "gffnT")
            if stage < 6:
                continue
            for f in range(NF):
                wg, wgk = ring_load(wF[44 + f * 2 + 0])
                wu, wuk = ring_load(wF[44 + f * 2 + 1])
                bg = proj_F(wg, wgk, xnT_kc, ["xnT"])
                bu = proj_F(wu, wuk, xnT_kc, ["xnT"])
                act(sil, pb[bg][:, :], AF.Silu, ["pb%d" % bg], ["sil"])
                tt(ARENA[:, ACT_OFF + f * 512: ACT_OFF + (f + 1) * 512], sil, pb[bu][:, :], ALU.mult,
                   ["sil", "pb%d" % bu], ["actT%d" % f])
            for f in range(NF):
                wd, wdk = ring_load(w_down[f * 128:(f + 1) * 128, :])
                for s in range(4):
                    for hf in range(2):
                        bi = s * 2 + hf
                        mm(pb[bi][:, :], ARENA[:, ACT_OFF + f * 512 + s * 128: ACT_OFF + f * 512 + (s + 1) * 128],
                           wd[:, hf * 512:(hf + 1) * 512], f == 0, f == NF - 1, ["actT%d" % f, wdk], ["pb%d" % bi])
            for s in range(4):
                for hf in range(2):
                    bi = s * 2 + hf
                    xs = xt[:, s * 1024 + hf * 512: s * 1024 + (hf + 1) * 512]
                    tt(xs, xs, pb[bi][:, :], ALU.add, ["xt%d" % s, "pb%d" % bi], ["xt%d" % s])
                xs = xt[:, s * 1024:(s + 1) * 1024]
                memset(ss[:, s:s + 1], 0.0, "ss", eng="dve")
                act(outb[:, :], xs, AF.Square, ["xt%d" % s], ["outb", "ss"], accum_out=ss[:, s:s + 1])
                act(rstd[:, s:s + 1], ss[:, s:s + 1], AF.Sqrt, ["ss"], ["rstd"], bias=EPS, scale=1.0 / DM)
                S.add("dve", lambda e, s=s: e.reciprocal(out=rstd[:, s:s + 1], in_=rstd[:, s:s + 1]),
                      reads=["rstd"], writes=["rstd"])
                stt(outb[:, :], xs, rstd[:, s:s + 1], gfin[:, :], ALU.mult, ALU.mult, ["xt%d" % s, "rstd", "gfin"], ["outb"])
                r0 = (mt * 4 + s) * 128
                dma_sp(out[r0:r0 + 128, :], outb[:, :], "outst", reads=["outb"])

        if dbg is not None:
            name, shape, apfn = dbg
            S.barrier(["sp"])
            dma_sp(dbg_out, apfn(locals()), "dbgst")
        S.emit(nc)
    return nc


def _tileF(w):
    k = w.shape[0] // 128
    return np.ascontiguousarray(w.reshape(k, 128, w.shape[1]).transpose(1, 0, 2).reshape(128, k * w.shape[1]))


def _host_consts():
    ident = np.eye(128, dtype=np.float32)
    eneg = np.zeros((64, 4096), np.float32)
    for jb in range(64):
        eneg[jb, jb * 64:(jb + 1) * 64] = -30000.0
    cs = np.arange(256)[:, None] * 16
    s2 = np.arange(64)[None, :] * 64
    ovm = np.maximum(0, np.minimum(cs + 32, s2 + 64) - np.maximum(cs, s2)).astype(np.float32) / 32.0
    ovm[255] = 0.0
    ov = np.ascontiguousarray(ovm.reshape(2, 128, 64).transpose(1, 0, 2).reshape(128, 128))
    k = np.arange(128)[:, None]
    q = np.arange(128)[None, :]
    tri = (k <= q).astype(np.float32)
    win4 = (k > q).astype(np.float32)
    return ident, eneg, ov, tri, win4


def _core_consts(p):
    t = np.arange(NOWN)
    tg = t + (NOWN if p == 1 else 0)
    tl = t + NOWN
    i = np.arange(256)[:, None]
    end_l = 16 * i + 31
    ok = (end_l <= tl[None, :]) & (i < 255)
    if p == 0:
        ok &= (16 * i >= NOWN)
    cm = ok.astype(np.float32)
    cmpm = np.ascontiguousarray(cm.reshape(2, 128, NOWN).transpose(1, 0, 2).reshape(128, 2 * NOWN))
    jl = np.arange(64)[None, :]
    cur_l = (tl // 64)[:, None]
    first_l = 0 if p == 1 else 32
    bias = np.zeros((NOWN, 64), np.float32)
    bias[np.broadcast_to(jl > cur_l, bias.shape)] = -1.0e30
    if p == 0:
        bias[:, :32] = -1.0e30
    m = (jl == cur_l - 1) & (jl >= first_l)
    bias[np.broadcast_to(m, bias.shape)] = 1.0e30
    bias[np.broadcast_to(jl == cur_l, bias.shape)] = 2.0e30
    bias[:, first_l] = 3.0e30
    biasd = np.ascontiguousarray(bias.reshape(16, 128, 64).transpose(1, 0, 2).reshape(128, 1024))
    valid = np.ones(NCTX, np.float32)
    if p == 0:
        valid[:NOWN] = 0.0
    valid = np.ascontiguousarray(valid.reshape(32, 128).T)
    return cmpm, biasd, valid


def _prep_shared(w_in, conv_w, w_conv_out, cmp_pos_k, cmp_w1_k, cmp_w2_k, cmp_pos_v, cmp_w1_v, cmp_w2_v,
                 w_attn_out, w_o, g_mix, g_ffn, w_gate, w_up, w_down, g_final):
    f = lambda a: np.asarray(a, dtype=np.float32)
    w_in = f(w_in)[0]
    oB, oC, oH, oQ, oKC, oVC, oKS, oVS, oKW, oVW, oBR, oGC, oGA = (
        0, 512, 1024, 1536, 2560, 2816, 3072, 3328, 3584, 3840, 4096, 4144, 5168)
    cols = lambda o, n: w_in[:, o:o + n]
    kvF = []
    for o in (oKS, oKW, oKC, oVC):
        for c in range(2):
            kvF.append(_tileF(cols(o + c * 128, 128)))
    wkvF = np.stack(kvF)
    wkvT = _tileF(np.concatenate([cols(oVS, 256), cols(oVW, 256)], axis=1))
    tiles = []
    for pair in range(2):
        for g in range(4):
            h0 = (2 * pair) * 4 + g
            h1 = (2 * pair + 1) * 4 + g
            tiles.append(_tileF(np.concatenate([cols(oQ + h0 * 64, 64), cols(oQ + h1 * 64, 64)], axis=1)))
    for cc in range(4):
        tiles.append(_tileF(cols(oC + cc * 128, 128)))
        tiles.append(_tileF(cols(oH + cc * 128, 128)))
        tiles.append(_tileF(cols(oB + cc * 128, 128)))
    wao = f(w_attn_out)[0]
    for jc in range(8):
        tiles.append(_tileF(cols(oGC + jc * 128, 128)))
        tiles.append(_tileF(wao[:, jc * 128:(jc + 1) * 128]))
        tiles.append(_tileF(cols(oGA + jc * 128, 128)))
    wg = f(w_gate)[0]
    wu = f(w_up)[0]
    for ff in range(NF):
        tiles.append(_tileF(wg[:, ff * 128:(ff + 1) * 128]))
        tiles.append(_tileF(wu[:, ff * 128:(ff + 1) * 128]))
    wF = np.stack(tiles)
    assert wF.shape[0] == 88
    wbr = _tileF(cols(oBR, 48))
    wc = f(w_conv_out)[0]
    wco = np.stack([_tileF(wc[:, jc * 128:(jc + 1) * 128]) for jc in range(8)])
    w1 = lambda w: np.ascontiguousarray(f(w)[0].reshape(32, 64, 256).transpose(1, 0, 2).reshape(64, 8192))
    w2k = f(cmp_w2_k)[0].reshape(2, 128, 64).transpose(1, 0, 2)
    w2k = np.ascontiguousarray(np.concatenate([w2k, w2k], axis=2).reshape(128, 256))
    w2v = np.ascontiguousarray(f(cmp_w2_v)[0].reshape(2, 128, 64).transpose(1, 0, 2).reshape(128, 128))
    ident, eneg, ov, tri, win4 = _host_consts()
    vecT = lambda g: np.ascontiguousarray(f(g).reshape(8, 128).T)
    return {
        "wkvF": wkvF, "wkvT": wkvT, "wF": wF, "wbr": wbr, "wco": wco,
        "w_o": np.ascontiguousarray(f(w_o)[0]), "w_down": np.ascontiguousarray(f(w_down)[0]),
        "w1k": w1(cmp_w1_k), "w1v": w1(cmp_w1_v), "w2k": w2k, "w2v": w2v,
        "posk": np.ascontiguousarray(f(cmp_pos_k)[0].T), "posv": np.ascontiguousarray(f(cmp_pos_v)[0].T),
        "convw": np.ascontiguousarray(f(conv_w)[0].reshape(3, 4, 128).transpose(2, 1, 0).reshape(128, 12)),
        "gmixT": vecT(g_mix), "gffnT": vecT(g_ffn), "gfin": np.ascontiguousarray(f(g_final).reshape(1, 1024)),
        "ident": ident, "eneg": eneg, "ov": ov, "tri": tri, "win4": win4,
    }


def make_in_maps(x, **weights):
    x = np.asarray(x, dtype=np.float32)
    shared = _prep_shared(**weights)
    cc = [_core_consts(0), _core_consts(1)]
    in_maps = []
    for c in range(NCORES):
        b, p = c // 2, c % 2
        if p == 1:
            xcz = np.ascontiguousarray(x[b])
        else:
            xcz = np.concatenate([np.zeros((NOWN, DM), np.float32), x[b, :NOWN]], axis=0)
        cmpm, biasd, valid = cc[p]
        d = dict(shared)
        d.update({"xc": xcz, "cmpm": cmpm, "biasd": biasd, "valid": valid})
        in_maps.append(d)
    return in_maps


_PROG = {}


def kernel(x, w_in, conv_w, w_conv_out, cmp_pos_k, cmp_w1_k, cmp_w2_k, cmp_pos_v, cmp_w1_v, cmp_w2_v,
           w_attn_out, w_o, g_mix, g_ffn, w_gate, w_up, w_down, g_final):
    in_maps = make_in_maps(x, w_in=w_in, conv_w=conv_w, w_conv_out=w_conv_out, cmp_pos_k=cmp_pos_k,
                           cmp_w1_k=cmp_w1_k, cmp_w2_k=cmp_w2_k, cmp_pos_v=cmp_pos_v, cmp_w1_v=cmp_w1_v,
                           cmp_w2_v=cmp_w2_v, w_attn_out=w_attn_out, w_o=w_o, g_mix=g_mix, g_ffn=g_ffn,
                           w_gate=w_gate, w_up=w_up, w_down=w_down, g_final=g_final)
    if "nc" not in _PROG:
        _PROG["nc"] = build_program()
    res = run_bass_kernel_spmd(_PROG["nc"], in_maps, core_ids=list(range(NCORES)))
    outf = np.empty((4, 4096, DM), np.float32)
    for c in range(NCORES):
        b, p = c // 2, c % 2
        outf[b, p * NOWN:(p + 1) * NOWN] = np.asarray(res.results[c]["out"], dtype=np.float32)
    return outf
```
